# Optimizing a Trainium2 kernel written in Bass

```python
import math
import jax, jax.numpy as jnp
from jax import lax
import numpy as np

D_MODEL = 1024
BATCH = 8
SEQ = 4096
DEPTH = 2

N_MIXERS = 2
N_ATTN_LAYERS = (DEPTH + 1) // 2
N_CONV_LAYERS = DEPTH // 2
RMS_EPS = 1e-6

HEAD_DIM = 64
N_Q_HEADS = D_MODEL // HEAD_DIM
N_KV_HEADS = 4
GROUP = N_Q_HEADS // N_KV_HEADS
WINDOW = 128
ROT_DIM = HEAD_DIM // 4
ROPE_THETA = 500000.0
QKV_DIM = (N_Q_HEADS + 2 * N_KV_HEADS) * HEAD_DIM
NEG_INF = -1e30

CONV_WIDTH = 3

PEER_HEADS = 8
N_KEYS = 128
N_EXPERTS = N_KEYS * N_KEYS
PEER_TOPK = 16
QUERY_DIM = 256
QUERY_HALF = QUERY_DIM // 2
TOKEN_BLOCK = 128

kernel_name = "hybrid_swa_sink_shortconv_peer"


def rmsnorm(x, gain):
    x32 = x.astype(jnp.float32)
    y = x32 * lax.rsqrt(jnp.mean(x32 * x32, axis=-1, keepdims=True) + RMS_EPS)
    return (y * gain.astype(jnp.float32)).astype(x.dtype)


def partial_rope(x, pos):
    half = ROT_DIM // 2
    freqs = ROPE_THETA ** (-jnp.arange(0, ROT_DIM, 2, dtype=jnp.float32) / ROT_DIM)
    ang = pos.astype(jnp.float32)[:, None] * freqs[None, :]
    cos = jnp.cos(ang)[None, :, None, :]
    sin = jnp.sin(ang)[None, :, None, :]
    x32 = x.astype(jnp.float32)
    x1, x2, rest = x32[..., :half], x32[..., half:ROT_DIM], x32[..., ROT_DIM:]
    out = jnp.concatenate([x1 * cos - x2 * sin, x2 * cos + x1 * sin, rest], axis=-1)
    return out.astype(x.dtype)


def sliding_window_attention(xn, w_qkv, q_gain, k_gain, sinks, w_o):
    B, S, _ = xn.shape
    nb = S // WINDOW
    qkv = xn @ w_qkv
    q_end = N_Q_HEADS * HEAD_DIM
    k_end = q_end + N_KV_HEADS * HEAD_DIM
    q = qkv[..., :q_end].reshape(B, S, N_Q_HEADS, HEAD_DIM)
    k = qkv[..., q_end:k_end].reshape(B, S, N_KV_HEADS, HEAD_DIM)
    v = qkv[..., k_end:].reshape(B, S, N_KV_HEADS, HEAD_DIM)
    q = rmsnorm(q, q_gain)
    k = rmsnorm(k, k_gain)
    pos = jnp.arange(S)
    q = partial_rope(q, pos)
    k = partial_rope(k, pos)

    qb = q.reshape(B, nb, WINDOW, N_KV_HEADS, GROUP, HEAD_DIM)
    pad = ((0, 0), (WINDOW, 0), (0, 0), (0, 0))
    kp = jnp.pad(k, pad).reshape(B, nb + 1, WINDOW, N_KV_HEADS, HEAD_DIM)
    vp = jnp.pad(v, pad).reshape(B, nb + 1, WINDOW, N_KV_HEADS, HEAD_DIM)
    kband = jnp.concatenate([kp[:, :-1], kp[:, 1:]], axis=2)
    vband = jnp.concatenate([vp[:, :-1], vp[:, 1:]], axis=2)

    scale = 1.0 / math.sqrt(HEAD_DIM)
    s = jnp.einsum('bnqhgd,bnkhd->bnhgqk', qb, kband).astype(jnp.float32) * scale
    qi = jnp.arange(WINDOW)[:, None]
    ki = jnp.arange(2 * WINDOW)[None, :]
    rel = WINDOW + qi - ki
    band = (rel >= 0) & (rel < WINDOW)
    blk = jnp.arange(nb)[:, None, None]
    valid = band[None] & (blk * WINDOW + ki[None] >= WINDOW)
    s = jnp.where(valid[None, :, None, None], s, NEG_INF)

    sink = sinks.astype(jnp.float32).reshape(N_KV_HEADS, GROUP)[None, None, :, :, None, None]
    sink = jnp.broadcast_to(sink, s.shape[:-1] + (1,))
    p = jax.nn.softmax(jnp.concatenate([s, sink], axis=-1), axis=-1)[..., :-1]
    o = jnp.einsum('bnhgqk,bnkhd->bnqhgd', p.astype(vband.dtype), vband)
    return o.reshape(B, S, N_Q_HEADS * HEAD_DIM) @ w_o


def short_gated_conv(xn, w_in, conv_w, w_out):
    B, S, D = xn.shape
    bcu = xn @ w_in
    gate_b = bcu[..., :D]
    gate_c = bcu[..., D:2 * D]
    u = bcu[..., 2 * D:]
    z = gate_c * u
    zp = jnp.pad(z, ((0, 0), (CONV_WIDTH - 1, 0), (0, 0)))
    conv = sum(conv_w[j] * zp[:, j:j + S] for j in range(CONV_WIDTH))
    return (gate_b * conv) @ w_out


def peer_ffn(xn, w_query, sub_keys, expert_u, expert_v):
    B, S, D = xn.shape
    T = B * S
    xt = xn.reshape(T, D)
    q = (xt @ w_query).reshape(T, PEER_HEADS, 2, QUERY_HALF)
    s = jnp.einsum('thpd,hpnd->thpn', q, sub_keys).astype(jnp.float32)
    s_top, i_top = lax.top_k(s, PEER_TOPK)
    cand = (s_top[:, :, 0, :, None] + s_top[:, :, 1, None, :]).reshape(T, PEER_HEADS, PEER_TOPK * PEER_TOPK)
    cand_idx = (i_top[:, :, 0, :, None] * N_KEYS + i_top[:, :, 1, None, :]).reshape(T, PEER_HEADS, PEER_TOPK * PEER_TOPK)
    g_s, sel = lax.top_k(cand, PEER_TOPK)
    idx = jnp.take_along_axis(cand_idx, sel, axis=-1)
    g = jax.nn.softmax(g_s, axis=-1)

    nblk = T // TOKEN_BLOCK

    def expert_block(args):
        xc, ic, gc = args
        u = jnp.take(expert_u, ic, axis=0)
        a = jnp.einsum('td,thkd->thk', xc, u).astype(jnp.float32)
        w = (gc * jax.nn.gelu(a, approximate=False)).astype(xc.dtype)
        v = jnp.take(expert_v, ic, axis=0)
        return jnp.einsum('thk,thkd->td', w, v)

    y = lax.map(expert_block, (xt.reshape(nblk, TOKEN_BLOCK, D),
                               idx.reshape(nblk, TOKEN_BLOCK, PEER_HEADS, PEER_TOPK),
                               g.reshape(nblk, TOKEN_BLOCK, PEER_HEADS, PEER_TOPK)))
    return y.reshape(B, S, D)


def setup_inputs(seed: int = 0) -> dict:
    key = jax.random.key(seed)
    ks = jax.random.split(key, 16)
    D = D_MODEL
    nrm = jax.random.normal
    x = nrm(ks[0], (BATCH, SEQ, D), jnp.float32)
    norm_mix = 1.0 + 0.05 * nrm(ks[1], (DEPTH, D), jnp.float32)
    norm_ffn = 1.0 + 0.05 * nrm(ks[2], (DEPTH, D), jnp.float32)
    attn_w_qkv = nrm(ks[3], (N_ATTN_LAYERS, D, QKV_DIM), jnp.float32) * D ** -0.5
    attn_q_norm = 1.0 + 0.05 * nrm(ks[4], (N_ATTN_LAYERS, HEAD_DIM), jnp.float32)
    attn_k_norm = 1.0 + 0.05 * nrm(ks[5], (N_ATTN_LAYERS, HEAD_DIM), jnp.float32)
    attn_sinks = 0.5 * nrm(ks[6], (N_ATTN_LAYERS, N_Q_HEADS), jnp.float32)
    attn_w_o = nrm(ks[7], (N_ATTN_LAYERS, N_Q_HEADS * HEAD_DIM, D), jnp.float32) * (N_Q_HEADS * HEAD_DIM) ** -0.5
    conv_w_in = nrm(ks[8], (N_CONV_LAYERS, D, 3 * D), jnp.float32) * D ** -0.5
    conv_w = nrm(ks[9], (N_CONV_LAYERS, CONV_WIDTH, D), jnp.float32) * CONV_WIDTH ** -0.5
    conv_w_out = nrm(ks[10], (N_CONV_LAYERS, D, D), jnp.float32) * D ** -0.5
    peer_w_query = nrm(ks[11], (DEPTH, D, PEER_HEADS * QUERY_DIM), jnp.float32) * D ** -0.5
    peer_sub_keys = nrm(ks[12], (DEPTH, PEER_HEADS, 2, N_KEYS, QUERY_HALF), jnp.float32) * QUERY_HALF ** -0.5
    peer_u = nrm(ks[13], (DEPTH, N_EXPERTS, D), jnp.float32) * D ** -0.5
    peer_v = nrm(ks[14], (DEPTH, N_EXPERTS, D), jnp.float32) * (PEER_HEADS * PEER_TOPK) ** -0.5
    return {"x": x, "norm_mix": norm_mix, "norm_ffn": norm_ffn,
            "attn_w_qkv": attn_w_qkv, "attn_q_norm": attn_q_norm, "attn_k_norm": attn_k_norm,
            "attn_sinks": attn_sinks, "attn_w_o": attn_w_o,
            "conv_w_in": conv_w_in, "conv_w": conv_w, "conv_w_out": conv_w_out,
            "peer_w_query": peer_w_query, "peer_sub_keys": peer_sub_keys,
            "peer_u": peer_u, "peer_v": peer_v}


def reference(x, norm_mix, norm_ffn, attn_w_qkv, attn_q_norm, attn_k_norm, attn_sinks, attn_w_o,
              conv_w_in, conv_w, conv_w_out, peer_w_query, peer_sub_keys, peer_u, peer_v):
    for i in range(DEPTH):
        h = rmsnorm(x, norm_mix[i])
        j = i // N_MIXERS
        if i % N_MIXERS == 0:
            x = x + sliding_window_attention(h, attn_w_qkv[j], attn_q_norm[j], attn_k_norm[j],
                                             attn_sinks[j], attn_w_o[j])
        else:
            x = x + short_gated_conv(h, conv_w_in[j], conv_w[j], conv_w_out[j])
        h = rmsnorm(x, norm_ffn[i])
        x = x + peer_ffn(h, peer_w_query[i], peer_sub_keys[i], peer_u[i], peer_v[i])
    return x
```

```python
import contextlib
import numpy as np
import ml_dtypes
import concourse.bass as bass
import concourse.mybir as mybir
from concourse.bass_utils import run_bass_kernel_spmd

F32 = mybir.dt.float32
BF16 = mybir.dt.bfloat16
U32 = mybir.dt.uint32
I32 = mybir.dt.int32
AF = mybir.ActivationFunctionType
ALU = mybir.AluOpType
AX = mybir.AxisListType

D = 1024
SEQ = 4096
NB = SEQ // 128
NEXP = 16384
EPS = 1e-6


class Res:
    __slots__ = ("name", "w", "r", "dsem", "dcnt")

    def __init__(self, name):
        self.name = name
        self.w = None
        self.r = {}
        self.dsem = None
        self.dcnt = 0


class _Proxy:
    def __init__(self):
        self.call = None

    def __getattr__(self, name):
        def f(*a, **k):
            self.call = (name, a, k)
            return self
        return f


def _bind(fn):
    px = _Proxy()
    fn(px)
    name, a, k = px.call
    return lambda e: getattr(e, name)(*a, **k)


class FW:
    ENG = ("pe", "dve", "act", "pool", "sp")

    def __init__(self, nc, es):
        self.nc = nc
        self.es = es
        self.e = {"pe": nc.tensor, "dve": nc.vector, "act": nc.scalar,
                  "pool": nc.gpsimd, "sp": nc.sync}
        self.sem = {}
        self.cnt = {}
        self.seen = {k: {} for k in self.ENG}
        self.nsem = 0
        for k in self.ENG:
            self.sem[k] = self._newsem("prog_" + k)
            self.cnt[k] = 0
        self.dres = []
        self.nwaits = 0
        self.nins = 0
        self.rec = None

    def record(self):
        self.rec = []

    def stop(self):
        r, self.rec = self.rec, None
        return r

    _cool = [0]

    @staticmethod
    def replay(ops, k=None):
        if k is None:
            while ops:
                ops.pop(0)[1]()
            return
        n = 0
        non_dve = False
        if FW._cool[0] > 0:
            FW._cool[0] -= 1
            return
        while ops and n < k:
            eng = ops[0][0]
            if eng == "dve" and non_dve:
                FW._cool[0] = 2
                break
            if eng != "dve":
                non_dve = True
            ops.pop(0)[1]()
            n += 1

    def _newsem(self, name):
        self.nsem += 1
        return self.es.enter_context(self.nc.semaphore(name))

    def res(self, name, dma=False):
        r = Res(name)
        if dma:
            r.dsem = self._newsem("d%d_%s" % (self.nsem, name))
            self.dres.append(r)
        return r

    def _wait(self, eng, toks):
        need = {}
        for t in toks:
            if t is None:
                continue
            s, v, _ = t
            k = id(s)
            if k not in need or need[k][1] < v:
                need[k] = (s, v)
        seen = self.seen[eng]
        for k, (s, v) in need.items():
            if seen.get(k, 0) < v:
                self.e[eng].wait_ge(s, v)
                seen[k] = v
                self.nwaits += 1

    def _deps(self, ename, reads, writes):
        toks = []
        for r in reads:
            if r.w is not None:
                toks.append(r.w)
        for w in writes:
            if w.w is not None and not (ename == "pe" and w.w[2] == "pe"):
                toks.append(w.w)
            for t in w.r.values():
                toks.append(t)
        return toks

    def _commit(self, tok, reads, writes, key):
        for r in reads:
            r.r[key] = tok
        for w in writes:
            w.w = tok
            w.r = {}

    def op(self, eng, reads, writes, fn):
        if self.rec is not None:
            fn2 = _bind(fn)
            self.rec.append((eng, lambda: self.op(eng, list(reads), list(writes), fn2)))
            return
        self._wait(eng, self._deps(eng, reads, writes))
        ins = fn(self.e[eng])
        self.cnt[eng] += 1
        ins.then_inc(self.sem[eng], 1)
        self.nins += 1
        tok = (self.sem[eng], self.cnt[eng], eng)
        self._commit(tok, reads, writes, eng)
        return tok

    def dma(self, q, semres, reads, writes, fn, multi=False):
        if self.rec is not None:
            fn2 = _bind(fn)
            self.rec.append(("dma", lambda: self.dma(q, semres, list(reads), list(writes), fn2, multi)))
            return
        deps = self._deps("dma-issue", reads, writes)
        if multi:
            deps = [t for t in deps if t[0] is not semres.dsem]
        self._wait(q, deps)
        ins = fn(self.e[q])
        semres.dcnt += 16
        ins.then_inc(semres.dsem, 16)
        self.nins += 1
        tok = (semres.dsem, semres.dcnt, "dma")
        key = ("dma", id(semres.dsem))
        for r in reads:
            r.r[key] = tok
        for w in writes:
            w.w = tok
            if not multi:
                w.r = {}
        return tok

    def barrier(self):
        toks = [(self.sem[k], self.cnt[k], k) for k in self.ENG if self.cnt[k] > 0]
        toks += [(r.dsem, r.dcnt, "dma") for r in self.dres if r.dcnt > 0]
        for k in self.ENG:
            self._wait(k, toks)


def bc(ap, shape):
    return ap.broadcast_to(list(shape))


NG = 8
BATCH = 2


class Ctx:
    pass


def build(nblk=NB, layers=(0, 1), mixers=True, peer=True):
    nc = bass.Bass("TRN2", target_bir_lowering=False)
    seq = nblk * 128
    dram = {}

    def din(name, shape, dt=F32):
        dram[name] = nc.dram_tensor(name, list(shape), dt, kind="ExternalInput").ap()
        return dram[name]

    x_d = din("x", [seq, D])
    out_d = nc.dram_tensor("out", [seq, D], F32, kind="ExternalOutput").ap()
    xmid_d = nc.dram_tensor("xmid", [seq, D], F32, kind="Internal").ap()
    wqkv_d = din("wqkv", [D, 1536])
    wo_d = din("wo", [D, D])
    win_d = din("win", [D, 3072])
    wout_d = din("wout", [D, D])
    wq_d = [din("wq0", [D, 2048]), din("wq1", [D, 2048])]
    skT_d = [din("skT0", [128, 2048]), din("skT1", [128, 2048])]
    UV_d = [din("puv0", [NEXP, 2 * D]), din("puv1", [NEXP, 2 * D])]
    UVb_d = [nc.dram_tensor("uvb%d" % i, [NEXP, 2 * D], BF16, kind="Internal").ap() for i in range(2)]
    wqb_d = [nc.dram_tensor("wqb%d" % i, [D, 2048], BF16, kind="Internal").ap() for i in range(2)]
    gmixT_d = din("gmixT", [128, 2, 8])
    gffnT_d = din("gffnT", [128, 2, 8])
    gffn_d = din("gffn", [2, D])
    gqk_d = din("gqk", [1280])
    sinks_d = din("sinks", [16])
    cwT_d = din("cwT", [128, 3, 8])
    ident_d = din("ident", [128, 128], BF16)
    masks_d = din("masks", [128, 2, 128], BF16)
    cos_d = din("ropecos", [128, NB, 8])
    sin_d = din("ropesin", [128, NB, 8])
    iota_d = din("iota16", [128, 16])

    with contextlib.ExitStack() as es:
        fw = FW(nc, es)

        def sb(name, shape, dt, scope=es):
            return scope.enter_context(nc.sbuf_tensor("s_" + name, list(shape), dt))

        ps = es.enter_context(nc.psum_tensor("ps", [128, 8, 512], F32))
        bank = [fw.res("bank%d" % i) for i in range(8)]

        def ps_bf(b):
            return ps[:, b, :].bitcast(BF16).rearrange("p (k t) -> p k t", t=128)

        ident = sb("ident", [128, 128], BF16)
        iota16 = sb("iota16", [128, 16], F32)
        gmixT = sb("gmixT", [128, 2, 8], F32)
        gffnT = sb("gffnT", [128, 2, 8], F32)
        R_const = fw.res("const", dma=True)
        for t, d in ((ident, ident_d), (iota16, iota_d), (gmixT, gmixT_d), (gffnT, gffnT_d)):
            fw.dma("sp", R_const, [], [R_const], lambda e, t=t, d=d: e.dma_start(out=t[:], in_=d), multi=True)

        xin = [sb("xin0", [128, D], F32)] * 2
        R_xin = [fw.res("xin0", dma=True)] * 2
        x1b = [sb("x1b%d" % i, [128, D], F32) for i in range(2)]
        R_x1 = [fw.res("x1b%d" % i, dma=True) for i in range(2)]
        stat = sb("stat", [128, 8], F32); R_stat = fw.res("stat")
        xs_bf = sb("xs_bf", [128, D], BF16); R_xs = fw.res("xs")
        xnT = sb("xnT", [128, 8, 128], BF16); R_xnT = fw.res("xnT")
        xn = [sb("xn%d" % i, [128, D], F32) for i in range(2)]
        R_xn = [fw.res("xn%d" % i) for i in range(2)]
        gffn_b = sb("gffn_b", [128, D], F32); R_gffn = fw.res("gffn_b", dma=True)
        qT_bf = sb("qT_bf", [128, 16, 128], BF16); R_qT = [fw.res("qT%d" % i) for i in range(4)]
        sc = sb("sc", [128, 16, 128], F32); R_sc = [fw.res("sc%d" % i) for i in range(16)]
        stop = sb("stop", [128, 16, 16], F32); R_stop = [fw.res("stop%d" % i) for i in range(16)]
        itop_u = sb("itop_u", [128, 16, 16], U32); R_itu = [fw.res("itop_u%d" % i) for i in range(16)]
        itop_f = sb("itop_f", [128, 16, 16], F32); R_itf = fw.res("itop_f")
        cand = sc[:].rearrange("p a b -> p (a b)").rearrange("p (h c) -> p h c", c=256)
        gs = sb("gs", [128, 8, 16], F32); R_gs = [fw.res("gs%d" % i) for i in range(8)]
        pos_u = sb("pos_u", [128, 8, 16], U32); R_pos = [fw.res("pos_u%d" % i) for i in range(8)]
        k12u = sb("k12u", [128, 2, 128], U32); R_k12u = fw.res("k12u")
        k12f = sb("k12f", [128, 2, 128], F32); R_k12f = fw.res("k12f")
        e1 = sb("e1", [128, 8, 16, 16], BF16); R_e1 = fw.res("e1")
        selij = sb("selij", [128, 2, 128], F32); R_sel = fw.res("selij")
        eidf = sb("eidf", [128, 128], F32); R_eidf = fw.res("eidf")
        eid_u = [sb("eid_u%d" % i, [128, 128], U32) for i in range(2)]
        R_eid = [fw.res("eid%d" % i) for i in range(2)]
        gt = [sb("gt%d" % i, [128, 8, 16], F32) for i in range(2)]
        R_gt = [fw.res("gt%d" % i) for i in range(2)]
        gtmp = sb("gtmp", [128, 8, 16], F32); R_gtmp = fw.res("gtmp")
        gsum = sb("gsum", [128, 8, 2], F32); R_gsum = fw.res("gsum")
        a_t = sb("a_t", [128, 128], F32); R_a = [fw.res("a%d" % i) for i in range(128)]
        w_t = sb("w_t", [128, 128], F32); R_w = [fw.res("w%d" % i) for i in range(128)]
        NI = 4
        identg = [sb("identg%d" % i, [128, 8, 128], BF16) for i in range(NI)]
        R_identg = [fw.res("identg%d" % i) for i in range(NI)]
        ug = [sb("ug%d" % i, [128, 2 * D], BF16) for i in range(NG)]
        R_ug = [fw.res("ug%d" % i, dma=True) for i in range(NG)]
        wqs = [sb("wqs%d" % i, [128, 8, 256], BF16) for i in range(2)]
        R_wqs = [fw.res("wqs%d" % i, dma=True) for i in range(2)]
        ND = 4
        diag = [sb("diag%d" % i, [128, 128], BF16) for i in range(ND)]
        R_diag = [fw.res("diag%d" % i) for i in range(ND)]
        R_store = [fw.res("store%d" % i, dma=True) for i in range(2)]
        cnt = {"g": 0, "d": 0}

        def rms_T(X, R_X, gT_ap, R_g):
            fw.op("act", [R_X], [R_xs, R_stat],
                  lambda e: e.activation(out=xs_bf[:], in_=X, func=AF.Square, accum_out=stat[:, 0:1]))
            fw.op("dve", [R_stat], [R_stat],
                  lambda e: e.tensor_scalar(out=stat[:, 1:2], in0=stat[:, 0:1], scalar1=1.0 / D, scalar2=EPS,
                                            op0=ALU.mult, op1=ALU.add))
            fw.op("act", [R_stat], [R_stat],
                  lambda e: e.activation(out=stat[:, 3:4], in_=stat[:, 1:2], func=AF.Sqrt))
            fw.op("dve", [R_stat], [R_stat], lambda e: e.reciprocal(out=stat[:, 2:3], in_=stat[:, 3:4]))
            fw.op("act", [R_X, R_stat], [R_xs],
                  lambda e: e.activation(out=xs_bf[:], in_=X, func=AF.Copy, scale=stat[:, 2:3]))
            tr = ps_bf(0)
            for kc in range(8):
                fw.op("pe", [R_xs, R_const], [bank[0]],
                      lambda e, kc=kc: e.transpose(out=tr[:, kc, :], in_=xs_bf[:, kc * 128:(kc + 1) * 128],
                                                   identity=ident[:]))
            fw.op("dve", [bank[0], R_g], [R_xnT],
                  lambda e: e.tensor_tensor(out=xnT[:], in0=tr, in1=bc(gT_ap.unsqueeze(2), [128, 8, 128]),
                                            op=ALU.mult))

        def topk16_staged(items):
            for (src, Rs, vals, Rv, idx, Ri) in items:
                fw.op("dve", Rs, [Rv], lambda e: e.max(out=vals[:, 0:8], in_=src))
            for (src, Rs, vals, Rv, idx, Ri) in items:
                fw.op("dve", Rs + [Rv], [Ri], lambda e: e.max_index(out=idx[:, 0:8], in_max=vals[:, 0:8], in_values=src))
            for (src, Rs, vals, Rv, idx, Ri) in items:
                fw.op("dve", Rs + [Rv], Rs,
                      lambda e: e.match_replace(out=src, in_to_replace=vals[:, 0:8], in_values=src, imm_value=-1e30))
            for (src, Rs, vals, Rv, idx, Ri) in items:
                fw.op("dve", Rs + [Rv], [Rv], lambda e: e.max(out=vals[:, 8:16], in_=src))
            for (src, Rs, vals, Rv, idx, Ri) in items:
                fw.op("dve", Rs + [Rv, Ri], [Ri],
                      lambda e: e.max_index(out=idx[:, 8:16], in_max=vals[:, 8:16], in_values=src))

        def peer_pre(L, n, W):
            p = n % 2
            X = x1b[p][:]
            rms_T(X, R_x1[p], gffnT[:, L, :], R_const)
            fw.op("dve", [R_x1[p], R_stat, R_gffn], [R_xn[p]],
                  lambda e: e.scalar_tensor_tensor(out=xn[p][:], in0=X, scalar=stat[:, 2:3], in1=gffn_b[:],
                                                   op0=ALU.mult, op1=ALU.mult))
            wsrc = wqb_d[L].rearrange("(kc q) m -> q kc m", q=128)

            def wq_load(g):
                fw.dma("sp", R_wqs[g % 2], [], [R_wqs[g % 2]],
                       lambda e: e.dma_start(out=wqs[g % 2][:], in_=wsrc[:, :, g * 256:(g + 1) * 256]))
            for g in range(8):
                if g < 2:
                    wq_load(g)
                for hp in (2 * g, 2 * g + 1):
                    b = 1 + hp // 4
                    for kc in range(8):
                        fw.op("pe", [R_wqs[g % 2], R_xnT], [bank[b]],
                              lambda e: e.matmul(
                                  out=ps[:, b, (hp % 4) * 128:(hp % 4 + 1) * 128],
                                  lhsT=wqs[g % 2][:, kc, (hp % 2) * 128:(hp % 2 + 1) * 128], rhs=xnT[:, kc, :],
                                  start=(kc == 0), stop=(kc == 7)))
                if g + 2 < 8:
                    wq_load(g + 2)
            for c in range(4):
                fw.op("act", [bank[1 + c]], [R_qT[c]],
                      lambda e, c=c: e.activation(out=qT_bf[:, 4 * c:4 * c + 4, :].rearrange("p a b -> p (a b)"),
                                                  in_=ps[:, 1 + c, :], func=AF.Copy))
            sbanks = [5, 0, 1, 2]
            for hp in range(16):
                b = sbanks[hp // 4]
                fw.op("pe", [R_qT[hp // 4], W.R_kT], [bank[b]],
                      lambda e, hp=hp, b=b: e.matmul(out=ps[:, b, (hp % 4) * 128:(hp % 4 + 1) * 128],
                                                     lhsT=qT_bf[:, hp, :], rhs=W.kT[:, hp, :],
                                                     start=True, stop=True))
            for c in range(4):
                fw.op("act", [bank[sbanks[c]]], R_sc[4 * c:4 * c + 4],
                      lambda e, c=c: e.activation(out=sc[:, 4 * c:4 * c + 4, :].rearrange("p a b -> p (a b)"),
                                                  in_=ps[:, sbanks[c], :], func=AF.Copy))
            topk16_staged([(sc[:, hp, :], [R_sc[hp]], stop[:, hp, :], R_stop[hp], itop_u[:, hp, :], R_itu[hp])
                           for hp in range(16)])
            fw.op("dve", R_itu, [R_itf], lambda e: e.tensor_copy(out=itop_f[:], in_=itop_u[:]))
            st4 = stop[:].rearrange("p (h two) k -> p h two k", two=2)
            it4 = itop_f[:].rearrange("p (h two) k -> p h two k", two=2)
            fw.op("dve", R_stop, R_sc,
                  lambda e: e.tensor_tensor(out=cand.rearrange("p h (a b) -> p h a b", b=16),
                                            in0=bc(st4[:, :, 0, :].unsqueeze(3), [128, 8, 16, 16]),
                                            in1=bc(st4[:, :, 1, :].unsqueeze(2), [128, 8, 16, 16]), op=ALU.add))
            topk16_staged([(cand[:, h, :], [R_sc[2 * h], R_sc[2 * h + 1]], gs[:, h, :], R_gs[h], pos_u[:, h, :], R_pos[h])
                           for h in range(8)])
            posf = pos_u[:].rearrange("p h k -> p (h k)")
            fw.op("dve", R_pos, [R_k12u],
                  lambda e: e.tensor_single_scalar(out=k12u[:, 0, :], in_=posf, scalar=4,
                                                   op=ALU.logical_shift_right))
            fw.op("dve", R_pos + [R_k12u], [R_k12u],
                  lambda e: e.tensor_single_scalar(out=k12u[:, 1, :], in_=posf, scalar=15, op=ALU.bitwise_and))
            fw.op("dve", [R_k12u], [R_k12f], lambda e: e.tensor_copy(out=k12f[:], in_=k12u[:]))
            for pp in range(2):
                kf = k12f[:, pp, :].rearrange("p (h k) -> p h k", k=16)
                fw.op("dve", [R_k12f, R_const], [R_e1],
                      lambda e, kf=kf: e.tensor_tensor(
                          out=e1[:], in0=bc(kf.unsqueeze(3), [128, 8, 16, 16]),
                          in1=bc(iota16[:].unsqueeze(1).unsqueeze(1), [128, 8, 16, 16]), op=ALU.is_equal))
                fw.op("dve", [R_e1, R_itf], [R_e1],
                      lambda e, pp=pp: e.tensor_tensor(
                          out=e1[:], in0=e1[:], in1=bc(it4[:, :, pp, :].unsqueeze(2), [128, 8, 16, 16]),
                          op=ALU.mult))
                fw.op("dve", [R_e1], [R_sel],
                      lambda e, pp=pp: e.tensor_reduce(out=selij[:, pp, :].rearrange("p (h k) -> p h k", k=16),
                                                       in_=e1[:], axis=AX.X, op=ALU.add))
            fw.op("dve", [R_sel], [R_eidf],
                  lambda e: e.scalar_tensor_tensor(out=eidf[:], in0=selij[:, 0, :], scalar=128.0,
                                                   in1=selij[:, 1, :], op0=ALU.mult, op1=ALU.add))
            fw.op("dve", [R_eidf], [R_eid[p]], lambda e: e.tensor_copy(out=eid_u[p][:], in_=eidf[:]))
            fw.op("dve", R_gs, [R_gtmp],
                  lambda e: e.tensor_tensor(out=gtmp[:], in0=gs[:], in1=bc(gs[:, :, 0:1], [128, 8, 16]),
                                            op=ALU.subtract))
            fw.op("act", [R_gtmp], [R_gtmp], lambda e: e.activation(out=gtmp[:], in_=gtmp[:], func=AF.Exp))
            fw.op("dve", [R_gtmp], [R_gsum],
                  lambda e: e.tensor_reduce(out=gsum[:, :, 0:1], in_=gtmp[:], axis=AX.X, op=ALU.add))
            fw.op("dve", [R_gsum], [R_gsum], lambda e: e.reciprocal(out=gsum[:, :, 1:2], in_=gsum[:, :, 0:1]))
            fw.op("dve", [R_gtmp, R_gsum], [R_gt[p]],
                  lambda e: e.tensor_tensor(out=gt[p][:], in0=gtmp[:], in1=bc(gsum[:, :, 1:2], [128, 8, 16]),
                                            op=ALU.mult))

        def gather(tab, dst, R_dst, idx_col, R_idx):
            fw.dma("pool", R_dst, [R_idx], [R_dst],
                   lambda e: e.indirect_dma_start(out=dst[:], out_offset=None, in_=tab,
                                                  in_offset=bass.IndirectOffsetOnAxis(ap=idx_col, axis=0)))

        def peer_G(L, n, dst_d, nxt):
            p = n % 2
            per = max(3, (len(nxt) + 159) // 160)
            gflat = gt[p][:].rearrange("p h k -> p (h k)")

            def build_identg(c):
                fw.op("dve", [R_gt[p], R_const], [R_identg[c % NI]],
                      lambda e: e.tensor_tensor(out=identg[c % NI][:], in0=bc(ident[:].unsqueeze(1), [128, 8, 128]),
                                                in1=bc(gflat[:, 8 * c:8 * c + 8].unsqueeze(2), [128, 8, 128]),
                                                op=ALU.mult))

            build_identg(0)
            build_identg(1)
            for hk in range(128):
                if hk % 8 == 0 and hk + 16 < 128:
                    build_identg(hk // 8 + 2)
                s = cnt["g"] % NG
                cnt["g"] += 1
                gather(UVb_d[L], ug[s], R_ug[s], eid_u[p][:, hk:hk + 1], R_eid[p])
                fw.op("dve", [R_ug[s], R_xn[p]], [R_ug[s], R_a[hk]],
                      lambda e: e.scalar_tensor_tensor(
                          out=ug[s][:, 0:D], in0=ug[s][:, 0:D], scalar=1.0, in1=xn[p][:], op0=ALU.mult,
                          op1=ALU.mult, accum_out=a_t[:, hk:hk + 1]))
                FW.replay(nxt, per)
                fw.op("act", [R_a[hk]], [R_w[hk]],
                      lambda e: e.activation(out=w_t[:, hk:hk + 1], in_=a_t[:, hk:hk + 1], func=AF.Gelu))
                d = cnt["d"] % ND
                cnt["d"] += 1
                fw.op("act", [R_w[hk], R_identg[(hk // 8) % NI]], [R_diag[d]],
                      lambda e: e.activation(out=diag[d][:], in_=identg[(hk // 8) % NI][:, hk % 8, :], func=AF.Copy,
                                             scale=w_t[:, hk:hk + 1]))
                for hf in range(2):
                    fw.op("pe", [R_diag[d], R_ug[s]], [bank[6 + hf]],
                          lambda e: e.matmul(out=ps[:, 6 + hf, :], lhsT=diag[d][:],
                                             rhs=ug[s][:, D + hf * 512:D + (hf + 1) * 512],
                                             start=(hk == 0), stop=(hk == 127)))
                FW.replay(nxt, per)
                bg_tick()
            if nxt:
                print("  [pre ops left un-overlapped: %d]" % len(nxt))
            FW.replay(nxt)
            fw.op("dve", [bank[6], bank[7], R_x1[p]], [R_x1[p]],
                  lambda e: e.tensor_tensor(out=x1b[p][:], in0=ps[:, 6:8, :].rearrange("p b f -> p (b f)"),
                                            in1=x1b[p][:], op=ALU.add))
            fw.dma("sp", R_store[p], [R_x1[p]], [],
                   lambda e: e.dma_start(out=dst_d[n * 128:(n + 1) * 128, :], in_=x1b[p][:]))

        bg_jobs = []
        bg = {"step": 0, "pending": None}
        if peer:
            NS = 4
            stg = [sb("stg%d" % i, [128, 2 * D], BF16) for i in range(NS)]
            R_stg = [fw.res("stg%d" % i, dma=True) for i in range(NS)]
            R_sto = [fw.res("stgo%d" % i, dma=True) for i in range(NS)]
            jcount = [0]

            def conv_load(src_t, r0):
                q = jcount[0] % NS
                jcount[0] += 1
                for hf in range(2):
                    fw.dma("pool", R_stg[q], [], [R_stg[q]],
                           lambda e: e.dma_start(out=stg[q][:, hf * D:(hf + 1) * D],
                                                 in_=src_t[r0:r0 + 128, hf * D:(hf + 1) * D]),
                           multi=(hf == 1))
                return q

            def conv_store(q, dst_t, r0):
                fw.dma("sp", R_sto[q], [R_stg[q]], [],
                       lambda e: e.dma_start(out=dst_t[r0:r0 + 128, :], in_=stg[q][:]))

            def jobs_for(L):
                return ([(wq_d[L], wqb_d[L], c * 128) for c in range(D // 128)]
                        + [(UV_d[L], UVb_d[L], c * 128) for c in range(NEXP // 128)])

            for (src_t, dst_t, r0) in jobs_for(layers[0]):
                conv_store(conv_load(src_t, r0), dst_t, r0)
            for L in layers[1:]:
                bg_jobs += jobs_for(L)
            fw.barrier()

        def bg_tick(every=24):
            bg["step"] += 1
            if bg["step"] % every:
                return
            if bg["pending"] is not None:
                conv_store(*bg["pending"])
                bg["pending"] = None
            if bg_jobs:
                src_t, dst_t, r0 = bg_jobs.pop(0)
                bg["pending"] = (conv_load(src_t, r0), dst_t, r0)

        def bg_flush():
            while bg_jobs or bg["pending"] is not None:
                bg_tick(1)

        for L in layers:
            src_d = x_d if L == layers[0] else xmid_d
            dst_d = out_d if L == layers[-1] else xmid_d
            with contextlib.ExitStack() as ls:
                W = Ctx()
                W.kT = sb("kT%d" % L, [128, 16, 128], BF16, ls); W.R_kT = fw.res("kT", dma=True)
                for c in range(2):
                    fw.dma("pool", W.R_kT, [], [W.R_kT],
                           lambda e, c=c: e.dma_start(
                               out=W.kT[:, 8 * c:8 * c + 8, :].rearrange("p a b -> p (a b)"),
                               in_=skT_d[L][:, c * 1024:(c + 1) * 1024]), multi=True)
                fw.dma("sp", R_gffn, [], [R_gffn],
                       lambda e: e.dma_start(out=gffn_b[:], in_=gffn_d[L, :].partition_broadcast(128)))
                if mixers:
                    mix = (attn_setup if L == 0 else conv_setup)(fw, nc, sb, ls, dram, ps, bank, ps_bf, L, W,
                                                                 dict(ident=ident, R_const=R_const, gmixT=gmixT,
                                                                      rms_T=rms_T, xnT=xnT, R_xnT=R_xnT,
                                                                      junkf=None, R_junkf=None,
                                                                      stat=stat, R_stat=R_stat))
                def pre_ops(n):
                    fw.record()
                    p = n % 2
                    if mixers:
                        fw.dma("sp", R_xin[p], [], [R_xin[p]],
                               lambda e: e.dma_start(out=xin[p][:], in_=src_d[n * 128:(n + 1) * 128, :]))
                        mix(n, xin[p], R_xin[p], x1b[p], R_x1[p])
                    else:
                        fw.dma("sp", R_x1[p], [], [R_x1[p]],
                               lambda e: e.dma_start(out=x1b[p][:], in_=src_d[n * 128:(n + 1) * 128, :]))
                    if peer:
                        peer_pre(L, n, W)
                    else:
                        fw.dma("sp", R_store[p], [R_x1[p]], [],
                               lambda e: e.dma_start(out=dst_d[n * 128:(n + 1) * 128, :], in_=x1b[p][:]))
                    return fw.stop()

                FW.replay(pre_ops(0))
                for n in range(nblk):
                    nxt = pre_ops(n + 1) if n + 1 < nblk else []
                    if peer:
                        peer_G(L, n, dst_d, nxt)
                    else:
                        FW.replay(nxt)
                if L == layers[0]:
                    bg_flush()
                fw.barrier()
                print("layer", L, "sbuf bytes remaining", nc.sbuf_bytes_remaining)
        fw.barrier()
        build.stats = dict(nins=fw.nins, nwaits=fw.nwaits, nsem=fw.nsem)
    return nc


def attn_setup(fw, nc, sb, ls, dram, ps, bank, ps_bf, L, W, env):
    ident, R_const, gmixT = env["ident"], env["R_const"], env["gmixT"]
    xnT, R_xnT, rms_T = env["xnT"], env["R_xnT"], env["rms_T"]

    wqkv = sb("wqkv", [128, 8, 1536], BF16, ls); R_wqkv = fw.res("wqkv", dma=True)
    wo = sb("wo", [128, 8, 1024], BF16, ls); R_wo = fw.res("wo", dma=True)
    for kc in range(8):
        for c0, c1 in ((0, 768), (768, 1536)):
            fw.dma("pool", R_wqkv, [], [R_wqkv],
                   lambda e, kc=kc, c0=c0, c1=c1: e.dma_start(out=wqkv[:, kc, c0:c1],
                                                              in_=dram["wqkv"][kc * 128:(kc + 1) * 128, c0:c1]),
                   multi=True)
        fw.dma("pool", R_wo, [], [R_wo],
               lambda e, kc=kc: e.dma_start(out=wo[:, kc, :], in_=dram["wo"][kc * 128:(kc + 1) * 128, :]),
               multi=True)
    gqk = sb("gqk", [128, 1280], F32, ls)
    cosT = sb("cosT", [128, NB, 8], F32, ls)
    sinT = sb("sinT", [128, NB, 8], F32, ls)
    masks = sb("masks", [128, 2, 128], BF16, ls)
    esink = sb("esink", [128, 16], F32, ls)
    R_ac = fw.res("attn_const", dma=True)
    fw.dma("sp", R_ac, [], [R_ac], lambda e: e.dma_start(out=gqk[:], in_=dram["gqk"].partition_broadcast(128)), multi=True)
    fw.dma("sp", R_ac, [], [R_ac], lambda e: e.dma_start(out=cosT[:], in_=dram["ropecos"]), multi=True)
    fw.dma("sp", R_ac, [], [R_ac], lambda e: e.dma_start(out=sinT[:], in_=dram["ropesin"]), multi=True)
    fw.dma("sp", R_ac, [], [R_ac], lambda e: e.dma_start(out=masks[:], in_=dram["masks"]), multi=True)
    fw.dma("sp", R_ac, [], [R_ac], lambda e: e.dma_start(out=esink[:], in_=dram["sinks"].partition_broadcast(128)), multi=True)
    R_esink = fw.res("esink")
    fw.op("act", [R_ac], [R_esink], lambda e: e.activation(out=esink[:], in_=esink[:], func=AF.Exp))

    qkn = sb("qkn", [128, 20, 64], F32, ls); R_qkn = fw.res("qkn")
    junkf, R_junkf = qkn[:].rearrange("p h d -> p (h d)"), R_qkn
    st20 = sb("st20", [128, 3, 20], F32, ls); R_st20 = fw.res("st20")
    rtmp = sb("rtmp", [128, 4, 20, 8], F32, ls); R_rtmp = fw.res("rtmp")
    qk_bf = sb("qk_bf", [128, 20, 64], BF16, ls); R_qkbf = fw.res("qk_bf")
    kdup = sb("kdup", [128, 2, 4, 128], BF16, ls); R_kdup = fw.res("kdup")
    qT = sb("qT", [128, 8, 128], BF16, ls); R_qT = fw.res("qT")
    kT = [sb("kTa%d" % i, [128, 8, 128], BF16, ls) for i in range(2)]
    R_kT = [fw.res("kTa%d" % i) for i in range(2)]
    vx = [sb("vx%d" % i, [128, 4, 66], BF16, ls) for i in range(2)]
    R_vx = [fw.res("vx%d" % i) for i in range(2)]
    es_t = sb("es_t", [128, 4, 128], BF16, ls); R_es = fw.res("es")
    pT = [sb("pT%d" % i, [128, 4, 128], BF16, ls) for i in range(4)]
    R_pT = [fw.res("pT%d" % i) for i in range(4)]
    o_bf = sb("o_bf", [128, 16, 64], BF16, ls); R_obf = fw.res("o_bf")
    oT = sb("oT", [128, 8, 128], BF16, ls); R_oT = fw.res("oT")
    den = sb("den", [128, 2, 4], F32, ls); R_den = fw.res("den")
    for i in range(2):
        fw.op("dve", [], [R_vx[i]], lambda e, i=i: e.memset(vx[i][:], 1.0))
    fw.op("dve", [], [R_kdup], lambda e: e.memset(kdup[:], 0.0))
    cnt = {"pT": 0}

    def mix(n, X, R_X, X1, R_X1):
        p = n % 2
        rms_T(X[:], R_X, gmixT[:, L, :], R_const)
        for c in range(3):
            for kc in range(8):
                fw.op("pe", [R_xnT, R_wqkv], [bank[2 + c]],
                      lambda e, c=c, kc=kc: e.matmul(out=ps[:, 2 + c, :], lhsT=xnT[:, kc, :],
                                                     rhs=wqkv[:, kc, c * 512:(c + 1) * 512],
                                                     start=(kc == 0), stop=(kc == 7)))
        psqk = ps[:, 2:5, :].rearrange("p b f -> p (b f)")
        qk3 = psqk[:, 0:1280].rearrange("p (h d) -> p h d", d=64)
        fw.op("act", [bank[4]], [R_vx[p]],
              lambda e: e.activation(out=vx[p][:, :, 0:64], in_=psqk[:, 1280:1536].rearrange("p (h d) -> p h d", d=64),
                                     func=AF.Copy))
        fw.op("act", [bank[2], bank[3], bank[4]], [R_junkf],
              lambda e: e.activation(out=junkf[:, 0:1280], in_=psqk[:, 0:1280], func=AF.Square))
        fw.op("dve", [R_junkf], [R_st20],
              lambda e: e.tensor_reduce(out=st20[:, 0, :], in_=junkf[:, 0:1280].rearrange("p (h d) -> p h d", d=64),
                                        axis=AX.X, op=ALU.add))
        fw.op("dve", [R_st20], [R_st20],
              lambda e: e.tensor_scalar(out=st20[:, 1, :], in0=st20[:, 0, :], scalar1=1.0 / 64, scalar2=EPS,
                                        op0=ALU.mult, op1=ALU.add))
        fw.op("act", [R_st20], [R_st20], lambda e: e.activation(out=st20[:, 2, :], in_=st20[:, 1, :], func=AF.Sqrt))
        fw.op("dve", [R_st20], [R_st20], lambda e: e.reciprocal(out=st20[:, 0, :], in_=st20[:, 2, :]))
        fw.op("dve", [bank[2], bank[3], bank[4], R_st20], [R_qkn],
              lambda e: e.tensor_tensor(out=qkn[:], in0=qk3, in1=bc(st20[:, 0, :].unsqueeze(2), [128, 20, 64]),
                                        op=ALU.mult))
        fw.op("dve", [R_qkn, R_ac], [R_qkn],
              lambda e: e.tensor_tensor(out=qkn[:], in0=qkn[:], in1=gqk[:].rearrange("p (h d) -> p h d", d=64),
                                        op=ALU.mult))
        fw.op("act", [R_qkn], [R_qkbf], lambda e: e.activation(out=qk_bf[:], in_=qkn[:], func=AF.Copy))
        cb = bc(cosT[:, n, :].unsqueeze(1), [128, 20, 8])
        sn = bc(sinT[:, n, :].unsqueeze(1), [128, 20, 8])
        x1v, x2v = qkn[:, :, 0:8], qkn[:, :, 8:16]
        for i, (a_, b_) in enumerate(((x1v, cb), (x2v, sn), (x2v, cb), (x1v, sn))):
            fw.op("dve", [R_qkn, R_ac], [R_rtmp],
                  lambda e, i=i, a_=a_, b_=b_: e.tensor_tensor(out=rtmp[:, i, :, :], in0=a_, in1=b_, op=ALU.mult))
        fw.op("dve", [R_rtmp, R_qkbf], [R_qkbf],
              lambda e: e.tensor_tensor(out=qk_bf[:, :, 0:8], in0=rtmp[:, 0, :, :], in1=rtmp[:, 1, :, :], op=ALU.subtract))
        fw.op("dve", [R_rtmp, R_qkbf], [R_qkbf],
              lambda e: e.tensor_tensor(out=qk_bf[:, :, 8:16], in0=rtmp[:, 2, :, :], in1=rtmp[:, 3, :, :], op=ALU.add))
        for hf in range(2):
            fw.op("act", [R_qkbf], [R_kdup],
                  lambda e, hf=hf: e.activation(out=kdup[:, hf, :, hf * 64:(hf + 1) * 64], in_=qk_bf[:, 16:20, :], func=AF.Copy))
        qflat = qk_bf[:].rearrange("p h d -> p (h d)")
        trq = ps_bf(1)
        for c in range(8):
            fw.op("pe", [R_qkbf, R_const], [bank[1]],
                  lambda e, c=c: e.transpose(out=trq[:, c, :], in_=qflat[:, c * 128:(c + 1) * 128], identity=ident[:]))
        fw.op("act", [bank[1]], [R_qT], lambda e: e.activation(out=qT[:], in_=trq, func=AF.Copy))
        trk = ps_bf(0)
        for c in range(8):
            fw.op("pe", [R_kdup, R_const], [bank[0]],
                  lambda e, c=c: e.transpose(out=trk[:, c, :], in_=kdup[:, c // 4, c % 4, :], identity=ident[:]))
        fw.op("dve", [bank[0]], [R_kT[p]], lambda e: e.tensor_copy(out=kT[p][:], in_=trk))
        chunks = ([(1 - p, 0)] if n > 0 else []) + [(p, 1)]
        for hkv in range(4):
            pslots = []
            for ci, (slot, mid) in enumerate(chunks):
                b = (5, 0)[ci]
                for g in range(4):
                    hf, pr = g % 2, g // 2
                    fw.op("pe", [R_kT[slot], R_qT], [bank[b]],
                          lambda e, g=g, hf=hf, pr=pr, slot=slot, b=b: e.matmul(
                              out=ps[:, b, g * 128:(g + 1) * 128], lhsT=kT[slot][:, hf * 4 + hkv, :],
                              rhs=qT[:, 2 * hkv + pr, :], start=True, stop=True))
                fw.op("act", [bank[b]], [R_es],
                      lambda e, b=b: e.activation(out=es_t[:].rearrange("p g q -> p (g q)"), in_=ps[:, b, :],
                                                  func=AF.Exp, scale=0.125))
                s = cnt["pT"] % 4
                cnt["pT"] += 1
                fw.op("dve", [R_es, R_ac], [R_pT[s]],
                      lambda e, s=s, mid=mid: e.tensor_tensor(out=pT[s][:], in0=es_t[:],
                                                              in1=bc(masks[:, mid, :].unsqueeze(1), [128, 4, 128]),
                                                              op=ALU.mult))
                pslots.append((s, slot))
            bo = 4 if hkv % 2 == 0 else 1
            for g in range(4):
                for ci, (s, slot) in enumerate(pslots):
                    fw.op("pe", [R_pT[s], R_vx[slot]], [bank[bo]],
                          lambda e, g=g, s=s, slot=slot, ci=ci: e.matmul(
                              out=ps[:, bo, g * 65:(g + 1) * 65], lhsT=pT[s][:, g, :], rhs=vx[slot][:, hkv, 0:65],
                              start=(ci == 0), stop=(ci == len(pslots) - 1)))
            o3 = ps[:, bo, 0:260].rearrange("p (g d) -> p g d", d=65)
            fw.op("dve", [bank[bo], R_esink], [R_den],
                  lambda e, o3=o3: e.tensor_tensor(out=den[:, 0, :].unsqueeze(2), in0=o3[:, :, 64:65],
                                                   in1=esink[:, 4 * hkv:4 * hkv + 4].unsqueeze(2), op=ALU.add))
            fw.op("dve", [R_den], [R_den], lambda e: e.reciprocal(out=den[:, 1, :], in_=den[:, 0, :]))
            fw.op("dve", [bank[bo], R_den], [R_obf],
                  lambda e, o3=o3: e.tensor_tensor(out=o_bf[:, 4 * hkv:4 * hkv + 4, :], in0=o3[:, :, 0:64],
                                                   in1=bc(den[:, 1, :].unsqueeze(2), [128, 4, 64]), op=ALU.mult))
        oflat = o_bf[:].rearrange("p h d -> p (h d)")
        tro = ps_bf(1)
        for c in range(8):
            fw.op("pe", [R_obf, R_const], [bank[1]],
                  lambda e, c=c: e.transpose(out=tro[:, c, :], in_=oflat[:, c * 128:(c + 1) * 128], identity=ident[:]))
        fw.op("act", [bank[1]], [R_oT], lambda e: e.activation(out=oT[:], in_=tro, func=AF.Copy))
        for hf in range(2):
            for c in range(8):
                fw.op("pe", [R_oT, R_wo], [bank[2 + hf]],
                      lambda e, hf=hf, c=c: e.matmul(out=ps[:, 2 + hf, :], lhsT=oT[:, c, :],
                                                     rhs=wo[:, c, hf * 512:(hf + 1) * 512],
                                                     start=(c == 0), stop=(c == 7)))
        fw.op("dve", [bank[2], bank[3], R_X], [R_X1],
              lambda e: e.tensor_tensor(out=X1[:], in0=ps[:, 2:4, :].rearrange("p b f -> p (b f)"), in1=X[:],
                                        op=ALU.add))
    return mix


def conv_setup(fw, nc, sb, ls, dram, ps, bank, ps_bf, L, W, env):
    ident, R_const, gmixT = env["ident"], env["R_const"], env["gmixT"]
    xnT, R_xnT, rms_T = env["xnT"], env["R_xnT"], env["rms_T"]

    win = sb("win", [128, 8, 3072], BF16, ls); R_win = fw.res("win", dma=True)
    wout = sb("wout", [128, 8, 1024], BF16, ls); R_wout = fw.res("wout", dma=True)
    for kc in range(8):
        for c in range(3):
            fw.dma("pool", R_win, [], [R_win],
                   lambda e, kc=kc, c=c: e.dma_start(out=win[:, kc, c * 1024:(c + 1) * 1024],
                                                     in_=dram["win"][kc * 128:(kc + 1) * 128, c * 1024:(c + 1) * 1024]),
                   multi=True)
        fw.dma("pool", R_wout, [], [R_wout],
               lambda e, kc=kc: e.dma_start(out=wout[:, kc, :], in_=dram["wout"][kc * 128:(kc + 1) * 128, :]),
               multi=True)
    cwT = sb("cwT", [128, 3, 8], F32, ls); R_cw = fw.res("cwT", dma=True)
    fw.dma("sp", R_cw, [], [R_cw], lambda e: e.dma_start(out=cwT[:], in_=dram["cwT"]))
    zb = [sb("zb%d" % i, [128, 8, 130], F32, ls) for i in range(2)]
    R_zb = [fw.res("zb%d" % i) for i in range(2)]
    u_sb = sb("u_sb", [128, 8, 128], F32, ls); R_u = fw.res("u_sb")
    t0, R_t0 = u_sb, R_u
    yb = sb("yb", [128, 8, 128], BF16, ls); R_yb = fw.res("yb")
    fw.op("dve", [], [R_zb[0]], lambda e: e.memset(zb[0][:], 0.0))

    def mix(n, X, R_X, X1, R_X1):
        p = n % 2
        rms_T(X[:], R_X, gmixT[:, L, :], R_const)
        for j in range(24):
            b = (1, 2, 3, 4, 5, 0)[j // 4]
            for kc in range(8):
                fw.op("pe", [R_xnT, R_win], [bank[b]],
                      lambda e, j=j, kc=kc, b=b: e.matmul(out=ps[:, b, (j % 4) * 128:(j % 4 + 1) * 128],
                                                          lhsT=win[:, kc, j * 128:(j + 1) * 128], rhs=xnT[:, kc, :],
                                                          start=(kc == 0), stop=(kc == 7)))
        v3 = lambda b0: ps[:, b0:b0 + 2, :].rearrange("p b (c t) -> p (b c) t", t=128)
        v1 = lambda b0: ps[:, b0, :].rearrange("p (c t) -> p c t", t=128)
        fw.op("act", [bank[5]], [R_u], lambda e: e.activation(out=u_sb[:, 0:4, :], in_=v1(5), func=AF.Copy))
        fw.op("act", [bank[0], R_u], [R_u], lambda e: e.activation(out=u_sb[:, 4:8, :], in_=v1(0), func=AF.Copy))
        fw.op("dve", [bank[3], bank[4], R_u], [R_zb[p]],
              lambda e: e.tensor_tensor(out=zb[p][:, :, 2:130], in0=v3(3), in1=u_sb[:], op=ALU.mult))
        fw.op("act", [R_zb[p]], [R_zb[1 - p]],
              lambda e: e.activation(out=zb[1 - p][:, :, 0:2], in_=zb[p][:, :, 128:130], func=AF.Copy))
        fw.op("dve", [R_zb[p], R_cw], [R_t0],
              lambda e: e.tensor_tensor(out=t0[:], in0=zb[p][:, :, 0:128],
                                        in1=bc(cwT[:, 0, :].unsqueeze(2), [128, 8, 128]), op=ALU.mult))
        for j in (1, 2):
            for cc in range(8):
                fw.op("dve", [R_zb[p], R_cw, R_t0], [R_t0],
                      lambda e, j=j, cc=cc: e.scalar_tensor_tensor(
                          out=t0[:, cc, :], in0=zb[p][:, cc, j:j + 128], scalar=cwT[:, j, cc:cc + 1],
                          in1=t0[:, cc, :], op0=ALU.mult, op1=ALU.add))
        fw.op("dve", [R_t0, bank[1], bank[2]], [R_yb],
              lambda e: e.tensor_tensor(out=yb[:], in0=v3(1), in1=t0[:], op=ALU.mult))
        for hf in range(2):
            for c in range(8):
                fw.op("pe", [R_yb, R_wout], [bank[(5, 0)[hf]]],
                      lambda e, hf=hf, c=c: e.matmul(out=ps[:, (5, 0)[hf], :], lhsT=yb[:, c, :],
                                                     rhs=wout[:, c, hf * 512:(hf + 1) * 512],
                                                     start=(c == 0), stop=(c == 7)))
        for hf in range(2):
            fw.op("dve", [bank[(5, 0)[hf]], R_X, R_X1], [R_X1],
                  lambda e, hf=hf: e.tensor_tensor(out=X1[:, hf * 512:(hf + 1) * 512], in0=ps[:, (5, 0)[hf], :],
                                                   in1=X[:, hf * 512:(hf + 1) * 512], op=ALU.add))
    return mix


def _consts():
    ident = np.eye(128, dtype=np.float32).astype(ml_dtypes.bfloat16)
    k = np.arange(128)[:, None]
    q = np.arange(128)[None, :]
    masks = np.stack([(k > q), (k <= q)], axis=1).astype(np.float32).astype(ml_dtypes.bfloat16)
    pos = (np.arange(NB)[None, :] * 128 + np.arange(128)[:, None]).astype(np.float32)
    freqs = (500000.0 ** (-np.arange(0, 16, 2, dtype=np.float32) / 16)).astype(np.float32)
    ang = pos[:, :, None] * freqs[None, None, :]
    iota16 = np.broadcast_to(np.arange(16, dtype=np.float32)[None, :], (128, 16)).copy()
    return dict(ident=ident, masks=np.ascontiguousarray(masks),
                ropecos=np.cos(ang).astype(np.float32), ropesin=np.sin(ang).astype(np.float32), iota16=iota16)


def make_in_maps(inp, ncores=8, nblk=NB):
    f = lambda a: np.ascontiguousarray(np.asarray(a, dtype=np.float32))
    colT = lambda g: np.ascontiguousarray(f(g).reshape(2, 8, 128).transpose(2, 0, 1))
    shared = dict(
        wqkv=f(inp["attn_w_qkv"][0]), wo=f(inp["attn_w_o"][0]),
        win=f(inp["conv_w_in"][0]), wout=f(inp["conv_w_out"][0]),
        wq0=f(inp["peer_w_query"][0]), wq1=f(inp["peer_w_query"][1]),
        skT0=np.ascontiguousarray(f(inp["peer_sub_keys"][0]).reshape(16, 128, 128).transpose(2, 0, 1).reshape(128, 2048)),
        skT1=np.ascontiguousarray(f(inp["peer_sub_keys"][1]).reshape(16, 128, 128).transpose(2, 0, 1).reshape(128, 2048)),
        puv0=np.concatenate([f(inp["peer_u"][0]), f(inp["peer_v"][0])], axis=1),
        puv1=np.concatenate([f(inp["peer_u"][1]), f(inp["peer_v"][1])], axis=1),
        gmixT=colT(inp["norm_mix"]), gffnT=colT(inp["norm_ffn"]), gffn=f(inp["norm_ffn"]),
        gqk=np.concatenate([np.tile(f(inp["attn_q_norm"][0]), 16), np.tile(f(inp["attn_k_norm"][0]), 4)]),
        sinks=f(inp["attn_sinks"][0]),
        cwT=np.ascontiguousarray(f(inp["conv_w"][0]).reshape(3, 8, 128).transpose(2, 0, 1)),
    )
    shared.update(_consts())
    x = f(inp["x"])
    maps = []
    for c in range(ncores):
        m = dict(shared)
        m["x"] = np.ascontiguousarray(x[c, :nblk * 128])
        maps.append(m)
    return maps


def kernel(**inputs):
    nc = build()
    maps = make_in_maps(inputs)
    res = run_bass_kernel_spmd(nc, maps, core_ids=list(range(8)))
    return np.stack([np.asarray(r["out"], dtype=np.float32) for r in res.results], axis=0)
```

```python
import contextlib
import numpy as np
import ml_dtypes
import concourse.bass as bass
import concourse.mybir as mybir
from concourse.bass_utils import run_bass_kernel_spmd

F32 = mybir.dt.float32
BF16 = mybir.dt.bfloat16
U32 = mybir.dt.uint32
I32 = mybir.dt.int32
AF = mybir.ActivationFunctionType
ALU = mybir.AluOpType
AX = mybir.AxisListType

D = 1024
SEQ = 4096
NB = SEQ // 128
NEXP = 16384
EPS = 1e-6


class Res:
    __slots__ = ("name", "w", "r", "dsem", "dcnt")

    def __init__(self, name):
        self.name = name
        self.w = None
        self.r = {}
        self.dsem = None
        self.dcnt = 0


class _Proxy:
    def __init__(self):
        self.call = None

    def __getattr__(self, name):
        def f(*a, **k):
            self.call = (name, a, k)
            return self
        return f


def _bind(fn):
    px = _Proxy()
    fn(px)
    name, a, k = px.call
    return lambda e: getattr(e, name)(*a, **k)


class FW:
    ENG = ("pe", "dve", "act", "pool", "sp")

    def __init__(self, nc, es):
        self.nc = nc
        self.es = es
        self.e = {"pe": nc.tensor, "dve": nc.vector, "act": nc.scalar,
                  "pool": nc.gpsimd, "sp": nc.sync}
        self.sem = {}
        self.cnt = {}
        self.seen = {k: {} for k in self.ENG}
        self.nsem = 0
        for k in self.ENG:
            self.sem[k] = self._newsem("prog_" + k)
            self.cnt[k] = 0
        self.dres = []
        self.nwaits = 0
        self.nins = 0
        self.rec = None

    def record(self):
        self.rec = []

    def stop(self):
        r, self.rec = self.rec, None
        return r

    _cool = [0]

    @staticmethod
    def replay(ops, k=None):
        if k is None:
            while ops:
                ops.pop(0)[1]()
            return
        n = 0
        non_dve = False
        if FW._cool[0] > 0:
            FW._cool[0] -= 1
            return
        while ops and n < k:
            eng = ops[0][0]
            if eng == "dve" and non_dve:
                FW._cool[0] = 1
                break
            if eng != "dve":
                non_dve = True
            ops.pop(0)[1]()
            n += 1

    def _newsem(self, name):
        self.nsem += 1
        return self.es.enter_context(self.nc.semaphore(name))

    def res(self, name, dma=False):
        r = Res(name)
        if dma:
            r.dsem = self._newsem("d%d_%s" % (self.nsem, name))
            self.dres.append(r)
        return r

    def _wait(self, eng, toks):
        need = {}
        for t in toks:
            if t is None:
                continue
            s, v, _ = t
            k = id(s)
            if k not in need or need[k][1] < v:
                need[k] = (s, v)
        seen = self.seen[eng]
        for k, (s, v) in need.items():
            if seen.get(k, 0) < v:
                self.e[eng].wait_ge(s, v)
                seen[k] = v
                self.nwaits += 1

    def _deps(self, ename, reads, writes):
        toks = []
        for r in reads:
            if r.w is not None:
                toks.append(r.w)
        for w in writes:
            if w.w is not None and not (ename == "pe" and w.w[2] == "pe"):
                toks.append(w.w)
            for t in w.r.values():
                toks.append(t)
        return toks

    def _commit(self, tok, reads, writes, key):
        for r in reads:
            r.r[key] = tok
        for w in writes:
            w.w = tok
            w.r = {}

    def op(self, eng, reads, writes, fn):
        if self.rec is not None:
            fn2 = _bind(fn)
            self.rec.append((eng, lambda: self.op(eng, list(reads), list(writes), fn2)))
            return
        self._wait(eng, self._deps(eng, reads, writes))
        ins = fn(self.e[eng])
        self.cnt[eng] += 1
        ins.then_inc(self.sem[eng], 1)
        self.nins += 1
        tok = (self.sem[eng], self.cnt[eng], eng)
        self._commit(tok, reads, writes, eng)
        return tok

    def dma(self, q, semres, reads, writes, fn, multi=False):
        if self.rec is not None:
            fn2 = _bind(fn)
            self.rec.append(("dma", lambda: self.dma(q, semres, list(reads), list(writes), fn2, multi)))
            return
        deps = self._deps("dma-issue", reads, writes)
        if multi:
            deps = [t for t in deps if t[0] is not semres.dsem]
        self._wait(q, deps)
        ins = fn(self.e[q])
        semres.dcnt += 16
        ins.then_inc(semres.dsem, 16)
        self.nins += 1
        tok = (semres.dsem, semres.dcnt, "dma")
        key = ("dma", id(semres.dsem))
        for r in reads:
            r.r[key] = tok
        for w in writes:
            w.w = tok
            if not multi:
                w.r = {}
        return tok

    def barrier(self):
        toks = [(self.sem[k], self.cnt[k], k) for k in self.ENG if self.cnt[k] > 0]
        toks += [(r.dsem, r.dcnt, "dma") for r in self.dres if r.dcnt > 0]
        for k in self.ENG:
            self._wait(k, toks)


def bc(ap, shape):
    return ap.broadcast_to(list(shape))


NG = 8
BATCH = 2


class Ctx:
    pass


def build(nblk=NB, layers=(0, 1), mixers=True, peer=True):
    nc = bass.Bass("TRN2", target_bir_lowering=False)
    seq = nblk * 128
    dram = {}

    def din(name, shape, dt=F32):
        dram[name] = nc.dram_tensor(name, list(shape), dt, kind="ExternalInput").ap()
        return dram[name]

    x_d = din("x", [seq, D])
    out_d = nc.dram_tensor("out", [seq, D], F32, kind="ExternalOutput").ap()
    xmid_d = nc.dram_tensor("xmid", [seq, D], F32, kind="Internal").ap()
    wqkv_d = din("wqkv", [D, 1536])
    wo_d = din("wo", [D, D])
    win_d = din("win", [D, 3072])
    wout_d = din("wout", [D, D])
    wq_d = [din("wq0", [D, 2048]), din("wq1", [D, 2048])]
    skT_d = [din("skT0", [128, 2048]), din("skT1", [128, 2048])]
    UV_d = [din("puv0", [NEXP, 2 * D]), din("puv1", [NEXP, 2 * D])]
    UVb_d = [nc.dram_tensor("uvb%d" % i, [NEXP, 2 * D], BF16, kind="Internal").ap() for i in range(2)]
    wqb_d = [nc.dram_tensor("wqb%d" % i, [D, 2048], BF16, kind="Internal").ap() for i in range(2)]
    gmixT_d = din("gmixT", [128, 2, 8])
    gffnT_d = din("gffnT", [128, 2, 8])
    gffn_d = din("gffn", [2, D])
    gqk_d = din("gqk", [1280])
    sinks_d = din("sinks", [16])
    cwT_d = din("cwT", [128, 3, 8])
    ident_d = din("ident", [128, 128], BF16)
    masks_d = din("masks", [128, 2, 128], BF16)
    cos_d = din("ropecos", [128, NB, 8])
    sin_d = din("ropesin", [128, NB, 8])
    iota_d = din("iota16", [128, 16])

    with contextlib.ExitStack() as es:
        fw = FW(nc, es)

        def sb(name, shape, dt, scope=es):
            return scope.enter_context(nc.sbuf_tensor("s_" + name, list(shape), dt))

        ps = es.enter_context(nc.psum_tensor("ps", [128, 8, 512], F32))
        bank = [fw.res("bank%d" % i) for i in range(8)]

        def ps_bf(b):
            return ps[:, b, :].bitcast(BF16).rearrange("p (k t) -> p k t", t=128)

        ident = sb("ident", [128, 128], BF16)
        iota16 = sb("iota16", [128, 16], F32)
        gmixT = sb("gmixT", [128, 2, 8], F32)
        gffnT = sb("gffnT", [128, 2, 8], F32)
        R_const = fw.res("const", dma=True)
        for t, d in ((ident, ident_d), (iota16, iota_d), (gmixT, gmixT_d), (gffnT, gffnT_d)):
            fw.dma("sp", R_const, [], [R_const], lambda e, t=t, d=d: e.dma_start(out=t[:], in_=d), multi=True)

        c4u = sb("c4u", [128, 128], U32); c15u = sb("c15u", [128, 128], U32)
        zu = sb("zu", [128, 256], U32); zf = sb("zf", [128, 128], F32)
        epsT = sb("epsT", [128, 20], F32); onesT = sb("onesT", [128, 20], F32)
        R_cst2 = fw.res("cst2")
        for t_, v_ in ((c4u, 4), (c15u, 15), (zu, 0), (zf, 0.0), (epsT, EPS), (onesT, 1.0)):
            fw.op("dve", [], [R_cst2], lambda e, t_=t_, v_=v_: e.memset(t_[:], v_))
        xin = [sb("xin0", [128, D], F32)] * 2
        R_xin = [fw.res("xin0", dma=True)] * 2
        x1b = [sb("x1b%d" % i, [128, D], F32) for i in range(2)]
        R_x1 = [fw.res("x1b%d" % i, dma=True) for i in range(2)]
        stat = sb("stat", [128, 8], F32); R_stat = fw.res("stat")
        xs_bf = sb("xs_bf", [128, D], BF16); R_xs = fw.res("xs")
        xnT = sb("xnT", [128, 8, 128], BF16); R_xnT = fw.res("xnT")
        xn = [sb("xn%d" % i, [128, D], F32) for i in range(2)]
        R_xn = [fw.res("xn%d" % i) for i in range(2)]
        gffn_b = sb("gffn_b", [128, D], F32); R_gffn = fw.res("gffn_b", dma=True)
        qT_bf = sb("qT_bf", [128, 16, 128], BF16); R_qT = [fw.res("qT%d" % i) for i in range(4)]
        sc = sb("sc", [128, 16, 128], F32); R_sc = [fw.res("sc%d" % i) for i in range(16)]
        stop = sb("stop", [128, 16, 16], F32); R_stop = [fw.res("stop%d" % i) for i in range(16)]
        itop_u = sb("itop_u", [128, 16, 16], U32); R_itu = [fw.res("itop_u%d" % i) for i in range(16)]
        itop_f = sb("itop_f", [128, 16, 16], F32); R_itf = fw.res("itop_f")
        cand = sc[:].rearrange("p a b -> p (a b)").rearrange("p (h c) -> p h c", c=256)
        gs = sb("gs", [128, 8, 16], F32); R_gs = [fw.res("gs%d" % i) for i in range(8)]
        pos_u = sb("pos_u", [128, 8, 16], U32); R_pos = [fw.res("pos_u%d" % i) for i in range(8)]
        k12u = sb("k12u", [128, 2, 128], U32); R_k12u = fw.res("k12u")
        k12f = sb("k12f", [128, 2, 128], F32); R_k12f = fw.res("k12f")
        e1 = sb("e1", [128, 8, 16, 16], BF16); R_e1 = fw.res("e1")
        selij = sb("selij", [128, 2, 128], F32); R_sel = fw.res("selij")
        eidf = sb("eidf", [128, 128], F32); R_eidf = fw.res("eidf")
        eid_u = [sb("eid_u%d" % i, [128, 128], U32) for i in range(2)]
        R_eid = [fw.res("eid%d" % i) for i in range(2)]
        gt = [sb("gt%d" % i, [128, 8, 16], F32) for i in range(2)]
        R_gt = [fw.res("gt%d" % i) for i in range(2)]
        gtmp = sb("gtmp", [128, 8, 16], F32); R_gtmp = fw.res("gtmp")
        gsum = sb("gsum", [128, 8, 2], F32); R_gsum = fw.res("gsum")
        a_t = sb("a_t", [128, 128], F32); R_a = [fw.res("a%d" % i) for i in range(128)]
        w_t = sb("w_t", [128, 128], F32); R_w = [fw.res("w%d" % i) for i in range(128)]
        NI = 4
        identg = [sb("identg%d" % i, [128, 8, 128], BF16) for i in range(NI)]
        R_identg = [fw.res("identg%d" % i) for i in range(NI)]
        ug = [sb("ug%d" % i, [128, 2 * D], BF16) for i in range(NG)]
        R_ug = [fw.res("ug%d" % i, dma=True) for i in range(NG)]
        wqs = [sb("wqs%d" % i, [128, 8, 256], BF16) for i in range(2)]
        R_wqs = [fw.res("wqs%d" % i, dma=True) for i in range(2)]
        ND = 4
        diag = [sb("diag%d" % i, [128, 128], BF16) for i in range(ND)]
        R_diag = [fw.res("diag%d" % i) for i in range(ND)]
        R_store = [fw.res("store%d" % i, dma=True) for i in range(2)]
        cnt = {"g": 0, "d": 0}

        def rms_T(X, R_X, gT_ap, R_g):
            fw.op("act", [R_X], [R_xs, R_stat],
                  lambda e: e.activation(out=xs_bf[:], in_=X, func=AF.Square, accum_out=stat[:, 0:1]))
            fw.op("dve", [R_stat, R_cst2], [R_stat],
                  lambda e: e.scalar_tensor_tensor(out=stat[:, 1:2], in0=stat[:, 0:1], scalar=1.0 / D,
                                                   in1=epsT[:, 0:1], op0=ALU.mult, op1=ALU.add))
            fw.op("act", [R_stat], [R_stat],
                  lambda e: e.activation(out=stat[:, 3:4], in_=stat[:, 1:2], func=AF.Sqrt))
            fw.op("dve", [R_stat], [R_stat], lambda e: e.reciprocal(out=stat[:, 2:3], in_=stat[:, 3:4]))
            fw.op("act", [R_X, R_stat], [R_xs],
                  lambda e: e.activation(out=xs_bf[:], in_=X, func=AF.Copy, scale=stat[:, 2:3]))
            tr = ps_bf(0)
            for kc in range(8):
                fw.op("pe", [R_xs, R_const], [bank[0]],
                      lambda e, kc=kc: e.transpose(out=tr[:, kc, :], in_=xs_bf[:, kc * 128:(kc + 1) * 128],
                                                   identity=ident[:]))
            fw.op("dve", [bank[0], R_g], [R_xnT],
                  lambda e: e.tensor_tensor(out=xnT[:], in0=tr, in1=bc(gT_ap.unsqueeze(2), [128, 8, 128]),
                                            op=ALU.mult))

        def topk16_staged(items):
            for (src, Rs, vals, Rv, idx, Ri) in items:
                fw.op("dve", Rs, [Rv], lambda e: e.max(out=vals[:, 0:8], in_=src))
            for (src, Rs, vals, Rv, idx, Ri) in items:
                fw.op("dve", Rs + [Rv], [Ri], lambda e: e.max_index(out=idx[:, 0:8], in_max=vals[:, 0:8], in_values=src))
            for (src, Rs, vals, Rv, idx, Ri) in items:
                fw.op("dve", Rs + [Rv], Rs,
                      lambda e: e.match_replace(out=src, in_to_replace=vals[:, 0:8], in_values=src, imm_value=-1e30))
            for (src, Rs, vals, Rv, idx, Ri) in items:
                fw.op("dve", Rs + [Rv], [Rv], lambda e: e.max(out=vals[:, 8:16], in_=src))
            for (src, Rs, vals, Rv, idx, Ri) in items:
                fw.op("dve", Rs + [Rv, Ri], [Ri],
                      lambda e: e.max_index(out=idx[:, 8:16], in_max=vals[:, 8:16], in_values=src))

        def peer_pre(L, n, W):
            p = n % 2
            X = x1b[p][:]
            rms_T(X, R_x1[p], gffnT[:, L, :], R_const)
            fw.op("dve", [R_x1[p], R_stat, R_gffn], [R_xn[p]],
                  lambda e: e.scalar_tensor_tensor(out=xn[p][:], in0=X, scalar=stat[:, 2:3], in1=gffn_b[:],
                                                   op0=ALU.mult, op1=ALU.mult))
            wsrc = wqb_d[L].rearrange("(kc q) m -> q kc m", q=128)

            def wq_load(g):
                fw.dma("sp", R_wqs[g % 2], [], [R_wqs[g % 2]],
                       lambda e: e.dma_start(out=wqs[g % 2][:], in_=wsrc[:, :, g * 256:(g + 1) * 256]))
            for g in range(8):
                if g < 2:
                    wq_load(g)
                for hp in (2 * g, 2 * g + 1):
                    b = 1 + hp // 4
                    for kc in range(8):
                        fw.op("pe", [R_wqs[g % 2], R_xnT], [bank[b]],
                              lambda e: e.matmul(
                                  out=ps[:, b, (hp % 4) * 128:(hp % 4 + 1) * 128],
                                  lhsT=wqs[g % 2][:, kc, (hp % 2) * 128:(hp % 2 + 1) * 128], rhs=xnT[:, kc, :],
                                  start=(kc == 0), stop=(kc == 7)))
                if g + 2 < 8:
                    wq_load(g + 2)
            for c in range(4):
                fw.op("act", [bank[1 + c]], [R_qT[c]],
                      lambda e, c=c: e.activation(out=qT_bf[:, 4 * c:4 * c + 4, :].rearrange("p a b -> p (a b)"),
                                                  in_=ps[:, 1 + c, :], func=AF.Copy))
            sbanks = [5, 0, 1, 2]
            for hp in range(16):
                b = sbanks[hp // 4]
                fw.op("pe", [R_qT[hp // 4], W.R_kT], [bank[b]],
                      lambda e, hp=hp, b=b: e.matmul(out=ps[:, b, (hp % 4) * 128:(hp % 4 + 1) * 128],
                                                     lhsT=qT_bf[:, hp, :], rhs=W.kT[:, hp, :],
                                                     start=True, stop=True))
            for c in range(4):
                fw.op("act", [bank[sbanks[c]]], R_sc[4 * c:4 * c + 4],
                      lambda e, c=c: e.activation(out=sc[:, 4 * c:4 * c + 4, :].rearrange("p a b -> p (a b)"),
                                                  in_=ps[:, sbanks[c], :], func=AF.Copy))
            topk16_staged([(sc[:, hp, :], [R_sc[hp]], stop[:, hp, :], R_stop[hp], itop_u[:, hp, :], R_itu[hp])
                           for hp in range(16)])
            fw.op("dve", R_itu + [R_cst2], [R_itf],
                  lambda e: e.tensor_tensor(out=itop_f[:].rearrange("p a b -> p (a b)"),
                                            in0=itop_u[:].rearrange("p a b -> p (a b)"), in1=zu[:], op=ALU.add))
            st4 = stop[:].rearrange("p (h two) k -> p h two k", two=2)
            it4 = itop_f[:].rearrange("p (h two) k -> p h two k", two=2)
            fw.op("dve", R_stop, R_sc,
                  lambda e: e.tensor_tensor(out=cand.rearrange("p h (a b) -> p h a b", b=16),
                                            in0=bc(st4[:, :, 0, :].unsqueeze(3), [128, 8, 16, 16]),
                                            in1=bc(st4[:, :, 1, :].unsqueeze(2), [128, 8, 16, 16]), op=ALU.add))
            topk16_staged([(cand[:, h, :], [R_sc[2 * h], R_sc[2 * h + 1]], gs[:, h, :], R_gs[h], pos_u[:, h, :], R_pos[h])
                           for h in range(8)])
            posf = pos_u[:].rearrange("p h k -> p (h k)")
            fw.op("dve", R_pos + [R_cst2], [R_k12u],
                  lambda e: e.tensor_tensor(out=k12u[:, 0, :], in0=posf, in1=c4u[:], op=ALU.logical_shift_right))
            fw.op("dve", R_pos + [R_k12u, R_cst2], [R_k12u],
                  lambda e: e.tensor_tensor(out=k12u[:, 1, :], in0=posf, in1=c15u[:], op=ALU.bitwise_and))
            fw.op("dve", [R_k12u, R_cst2], [R_k12f],
                  lambda e: e.tensor_tensor(out=k12f[:].rearrange("p a b -> p (a b)"),
                                            in0=k12u[:].rearrange("p a b -> p (a b)"), in1=zu[:], op=ALU.add))
            for pp in range(2):
                kf = k12f[:, pp, :].rearrange("p (h k) -> p h k", k=16)
                fw.op("dve", [R_k12f, R_const], [R_e1],
                      lambda e, kf=kf: e.tensor_tensor(
                          out=e1[:], in0=bc(kf.unsqueeze(3), [128, 8, 16, 16]),
                          in1=bc(iota16[:].unsqueeze(1).unsqueeze(1), [128, 8, 16, 16]), op=ALU.is_equal))
                fw.op("dve", [R_e1, R_itf], [R_e1],
                      lambda e, pp=pp: e.tensor_tensor(
                          out=e1[:], in0=e1[:], in1=bc(it4[:, :, pp, :].unsqueeze(2), [128, 8, 16, 16]),
                          op=ALU.mult))
                fw.op("dve", [R_e1], [R_sel],
                      lambda e, pp=pp: e.tensor_reduce(out=selij[:, pp, :].rearrange("p (h k) -> p h k", k=16),
                                                       in_=e1[:], axis=AX.X, op=ALU.add))
            fw.op("dve", [R_sel], [R_eidf],
                  lambda e: e.scalar_tensor_tensor(out=eidf[:], in0=selij[:, 0, :], scalar=128.0,
                                                   in1=selij[:, 1, :], op0=ALU.mult, op1=ALU.add))
            fw.op("dve", [R_eidf, R_cst2], [R_eid[p]],
                  lambda e: e.tensor_tensor(out=eid_u[p][:], in0=eidf[:], in1=zf[:], op=ALU.add))
            fw.op("dve", R_gs, [R_gtmp],
                  lambda e: e.tensor_tensor(out=gtmp[:], in0=gs[:], in1=bc(gs[:, :, 0:1], [128, 8, 16]),
                                            op=ALU.subtract))
            fw.op("act", [R_gtmp], [R_gtmp], lambda e: e.activation(out=gtmp[:], in_=gtmp[:], func=AF.Exp))
            fw.op("dve", [R_gtmp], [R_gsum],
                  lambda e: e.tensor_reduce(out=gsum[:, :, 0:1], in_=gtmp[:], axis=AX.X, op=ALU.add))
            fw.op("dve", [R_gsum], [R_gsum], lambda e: e.reciprocal(out=gsum[:, :, 1:2], in_=gsum[:, :, 0:1]))
            fw.op("dve", [R_gtmp, R_gsum], [R_gt[p]],
                  lambda e: e.tensor_tensor(out=gt[p][:], in0=gtmp[:], in1=bc(gsum[:, :, 1:2], [128, 8, 16]),
                                            op=ALU.mult))

        def gather(tab, dst, R_dst, idx_col, R_idx):
            fw.dma("pool", R_dst, [R_idx], [R_dst],
                   lambda e: e.indirect_dma_start(out=dst[:], out_offset=None, in_=tab,
                                                  in_offset=bass.IndirectOffsetOnAxis(ap=idx_col, axis=0)))

        def peer_G(L, n, dst_d, nxt):
            p = n % 2
            per = max(4, (len(nxt) + 79) // 80)
            gflat = gt[p][:].rearrange("p h k -> p (h k)")

            def build_identg(c):
                fw.op("dve", [R_gt[p], R_const], [R_identg[c % NI]],
                      lambda e: e.tensor_tensor(out=identg[c % NI][:], in0=bc(ident[:].unsqueeze(1), [128, 8, 128]),
                                                in1=bc(gflat[:, 8 * c:8 * c + 8].unsqueeze(2), [128, 8, 128]),
                                                op=ALU.mult))

            build_identg(0)
            build_identg(1)
            for hk in range(128):
                if hk % 8 == 0 and hk + 16 < 128:
                    build_identg(hk // 8 + 2)
                s = cnt["g"] % NG
                cnt["g"] += 1
                gather(UVb_d[L], ug[s], R_ug[s], eid_u[p][:, hk:hk + 1], R_eid[p])
                fw.op("dve", [R_ug[s], R_xn[p]], [R_ug[s], R_a[hk]],
                      lambda e: e.scalar_tensor_tensor(
                          out=ug[s][:, 0:D], in0=ug[s][:, 0:D], scalar=1.0, in1=xn[p][:], op0=ALU.mult,
                          op1=ALU.mult, accum_out=a_t[:, hk:hk + 1]))
                fw.op("act", [R_a[hk]], [R_w[hk]],
                      lambda e: e.activation(out=w_t[:, hk:hk + 1], in_=a_t[:, hk:hk + 1], func=AF.Gelu))
                d = cnt["d"] % ND
                cnt["d"] += 1
                fw.op("act", [R_w[hk], R_identg[(hk // 8) % NI]], [R_diag[d]],
                      lambda e: e.activation(out=diag[d][:], in_=identg[(hk // 8) % NI][:, hk % 8, :], func=AF.Copy,
                                             scale=w_t[:, hk:hk + 1]))
                for hf in range(2):
                    fw.op("pe", [R_diag[d], R_ug[s]], [bank[6 + hf]],
                          lambda e: e.matmul(out=ps[:, 6 + hf, :], lhsT=diag[d][:],
                                             rhs=ug[s][:, D + hf * 512:D + (hf + 1) * 512],
                                             start=(hk == 0), stop=(hk == 127)))
                FW.replay(nxt, per)
                bg_tick()
            if nxt:
                print("  [pre ops left un-overlapped: %d]" % len(nxt))
            FW.replay(nxt)
            fw.op("dve", [bank[6], bank[7], R_x1[p]], [R_x1[p]],
                  lambda e: e.tensor_tensor(out=x1b[p][:], in0=ps[:, 6:8, :].rearrange("p b f -> p (b f)"),
                                            in1=x1b[p][:], op=ALU.add))
            fw.dma("sp", R_store[p], [R_x1[p]], [],
                   lambda e: e.dma_start(out=dst_d[n * 128:(n + 1) * 128, :], in_=x1b[p][:]))

        bg_jobs = []
        bg = {"step": 0, "pending": None}
        if peer:
            NS = 4
            stg = [sb("stg%d" % i, [128, 2 * D], BF16) for i in range(NS)]
            R_stg = [fw.res("stg%d" % i, dma=True) for i in range(NS)]
            R_sto = [fw.res("stgo%d" % i, dma=True) for i in range(NS)]
            jcount = [0]

            def conv_load(src_t, r0):
                q = jcount[0] % NS
                jcount[0] += 1
                for hf in range(2):
                    fw.dma("pool", R_stg[q], [], [R_stg[q]],
                           lambda e: e.dma_start(out=stg[q][:, hf * D:(hf + 1) * D],
                                                 in_=src_t[r0:r0 + 128, hf * D:(hf + 1) * D]),
                           multi=(hf == 1))
                return q

            def conv_store(q, dst_t, r0):
                fw.dma("sp", R_sto[q], [R_stg[q]], [],
                       lambda e: e.dma_start(out=dst_t[r0:r0 + 128, :], in_=stg[q][:]))

            def jobs_for(L):
                return ([(wq_d[L], wqb_d[L], c * 128) for c in range(D // 128)]
                        + [(UV_d[L], UVb_d[L], c * 128) for c in range(NEXP // 128)])

            for (src_t, dst_t, r0) in jobs_for(layers[0]):
                conv_store(conv_load(src_t, r0), dst_t, r0)
            for L in layers[1:]:
                bg_jobs += jobs_for(L)
            fw.barrier()

        def bg_tick(every=24):
            bg["step"] += 1
            if bg["step"] % every:
                return
            if bg["pending"] is not None:
                conv_store(*bg["pending"])
                bg["pending"] = None
            if bg_jobs:
                src_t, dst_t, r0 = bg_jobs.pop(0)
                bg["pending"] = (conv_load(src_t, r0), dst_t, r0)

        def bg_flush():
            while bg_jobs or bg["pending"] is not None:
                bg_tick(1)

        for L in layers:
            src_d = x_d if L == layers[0] else xmid_d
            dst_d = out_d if L == layers[-1] else xmid_d
            with contextlib.ExitStack() as ls:
                W = Ctx()
                W.kT = sb("kT%d" % L, [128, 16, 128], BF16, ls); W.R_kT = fw.res("kT", dma=True)
                for c in range(2):
                    fw.dma("pool", W.R_kT, [], [W.R_kT],
                           lambda e, c=c: e.dma_start(
                               out=W.kT[:, 8 * c:8 * c + 8, :].rearrange("p a b -> p (a b)"),
                               in_=skT_d[L][:, c * 1024:(c + 1) * 1024]), multi=True)
                fw.dma("sp", R_gffn, [], [R_gffn],
                       lambda e: e.dma_start(out=gffn_b[:], in_=gffn_d[L, :].partition_broadcast(128)))
                if mixers:
                    mix = (attn_setup if L == 0 else conv_setup)(fw, nc, sb, ls, dram, ps, bank, ps_bf, L, W,
                                                                 dict(ident=ident, R_const=R_const, gmixT=gmixT,
                                                                      rms_T=rms_T, xnT=xnT, R_xnT=R_xnT,
                                                                      junkf=None, R_junkf=None,
                                                                      stat=stat, R_stat=R_stat, epsT=epsT,
                                                                      onesT=onesT, R_cst2=R_cst2))
                def pre_ops(n):
                    fw.record()
                    p = n % 2
                    if mixers:
                        fw.dma("sp", R_xin[p], [], [R_xin[p]],
                               lambda e: e.dma_start(out=xin[p][:], in_=src_d[n * 128:(n + 1) * 128, :]))
                        mix(n, xin[p], R_xin[p], x1b[p], R_x1[p])
                    else:
                        fw.dma("sp", R_x1[p], [], [R_x1[p]],
                               lambda e: e.dma_start(out=x1b[p][:], in_=src_d[n * 128:(n + 1) * 128, :]))
                    if peer:
                        peer_pre(L, n, W)
                    else:
                        fw.dma("sp", R_store[p], [R_x1[p]], [],
                               lambda e: e.dma_start(out=dst_d[n * 128:(n + 1) * 128, :], in_=x1b[p][:]))
                    return fw.stop()

                FW.replay(pre_ops(0))
                for n in range(nblk):
                    nxt = pre_ops(n + 1) if n + 1 < nblk else []
                    if peer:
                        peer_G(L, n, dst_d, nxt)
                    else:
                        FW.replay(nxt)
                if L == layers[0]:
                    bg_flush()
                fw.barrier()
                print("layer", L, "sbuf bytes remaining", nc.sbuf_bytes_remaining)
        fw.barrier()
        build.stats = dict(nins=fw.nins, nwaits=fw.nwaits, nsem=fw.nsem)
    return nc


def attn_setup(fw, nc, sb, ls, dram, ps, bank, ps_bf, L, W, env):
    ident, R_const, gmixT = env["ident"], env["R_const"], env["gmixT"]
    xnT, R_xnT, rms_T = env["xnT"], env["R_xnT"], env["rms_T"]
    epsT, onesT, R_cst2 = env["epsT"], env["onesT"], env["R_cst2"]

    wqkv = sb("wqkv", [128, 8, 1536], BF16, ls); R_wqkv = fw.res("wqkv", dma=True)
    wo = sb("wo", [128, 8, 1024], BF16, ls); R_wo = fw.res("wo", dma=True)
    for kc in range(8):
        for c0, c1 in ((0, 768), (768, 1536)):
            fw.dma("pool", R_wqkv, [], [R_wqkv],
                   lambda e, kc=kc, c0=c0, c1=c1: e.dma_start(out=wqkv[:, kc, c0:c1],
                                                              in_=dram["wqkv"][kc * 128:(kc + 1) * 128, c0:c1]),
                   multi=True)
        fw.dma("pool", R_wo, [], [R_wo],
               lambda e, kc=kc: e.dma_start(out=wo[:, kc, :], in_=dram["wo"][kc * 128:(kc + 1) * 128, :]),
               multi=True)
    gqk = sb("gqk", [128, 1280], F32, ls)
    cosT = sb("cosT", [128, NB, 8], F32, ls)
    sinT = sb("sinT", [128, NB, 8], F32, ls)
    masks = sb("masks", [128, 2, 128], BF16, ls)
    esink = sb("esink", [128, 16], F32, ls)
    R_ac = fw.res("attn_const", dma=True)
    fw.dma("sp", R_ac, [], [R_ac], lambda e: e.dma_start(out=gqk[:], in_=dram["gqk"].partition_broadcast(128)), multi=True)
    fw.dma("sp", R_ac, [], [R_ac], lambda e: e.dma_start(out=cosT[:], in_=dram["ropecos"]), multi=True)
    fw.dma("sp", R_ac, [], [R_ac], lambda e: e.dma_start(out=sinT[:], in_=dram["ropesin"]), multi=True)
    fw.dma("sp", R_ac, [], [R_ac], lambda e: e.dma_start(out=masks[:], in_=dram["masks"]), multi=True)
    fw.dma("sp", R_ac, [], [R_ac], lambda e: e.dma_start(out=esink[:], in_=dram["sinks"].partition_broadcast(128)), multi=True)
    R_esink = fw.res("esink")
    fw.op("act", [R_ac], [R_esink], lambda e: e.activation(out=esink[:], in_=esink[:], func=AF.Exp))

    qkn = sb("qkn", [128, 20, 64], F32, ls); R_qkn = fw.res("qkn")
    junkf, R_junkf = qkn[:].rearrange("p h d -> p (h d)"), R_qkn
    st20 = sb("st20", [128, 3, 20], F32, ls); R_st20 = fw.res("st20")
    rtmp = sb("rtmp", [128, 4, 20, 8], F32, ls); R_rtmp = fw.res("rtmp")
    qk_bf = sb("qk_bf", [128, 20, 64], BF16, ls); R_qkbf = fw.res("qk_bf")
    kdup = sb("kdup", [128, 2, 4, 128], BF16, ls); R_kdup = fw.res("kdup")
    qT = sb("qT", [128, 8, 128], BF16, ls); R_qT = fw.res("qT")
    kT = [sb("kTa%d" % i, [128, 8, 128], BF16, ls) for i in range(2)]
    R_kT = [fw.res("kTa%d" % i) for i in range(2)]
    vx = [sb("vx%d" % i, [128, 4, 66], BF16, ls) for i in range(2)]
    R_vx = [fw.res("vx%d" % i) for i in range(2)]
    es_t = sb("es_t", [128, 4, 128], BF16, ls); R_es = fw.res("es")
    pT = [sb("pT%d" % i, [128, 4, 128], BF16, ls) for i in range(4)]
    R_pT = [fw.res("pT%d" % i) for i in range(4)]
    o_bf = sb("o_bf", [128, 16, 64], BF16, ls); R_obf = fw.res("o_bf")
    oT = sb("oT", [128, 8, 128], BF16, ls); R_oT = fw.res("oT")
    den = sb("den", [128, 2, 4], F32, ls); R_den = fw.res("den")
    for i in range(2):
        fw.op("dve", [], [R_vx[i]], lambda e, i=i: e.memset(vx[i][:], 1.0))
    fw.op("dve", [], [R_kdup], lambda e: e.memset(kdup[:], 0.0))
    cnt = {"pT": 0}

    def mix(n, X, R_X, X1, R_X1):
        p = n % 2
        rms_T(X[:], R_X, gmixT[:, L, :], R_const)
        for c in range(3):
            for kc in range(8):
                fw.op("pe", [R_xnT, R_wqkv], [bank[2 + c]],
                      lambda e, c=c, kc=kc: e.matmul(out=ps[:, 2 + c, :], lhsT=xnT[:, kc, :],
                                                     rhs=wqkv[:, kc, c * 512:(c + 1) * 512],
                                                     start=(kc == 0), stop=(kc == 7)))
        psqk = ps[:, 2:5, :].rearrange("p b f -> p (b f)")
        qk3 = psqk[:, 0:1280].rearrange("p (h d) -> p h d", d=64)
        fw.op("act", [bank[4]], [R_vx[p]],
              lambda e: e.activation(out=vx[p][:, :, 0:64], in_=psqk[:, 1280:1536].rearrange("p (h d) -> p h d", d=64),
                                     func=AF.Copy))
        fw.op("act", [bank[2], bank[3], bank[4]], [R_junkf],
              lambda e: e.activation(out=junkf[:, 0:1280], in_=psqk[:, 0:1280], func=AF.Square))
        fw.op("dve", [R_junkf], [R_st20],
              lambda e: e.tensor_reduce(out=st20[:, 0, :], in_=junkf[:, 0:1280].rearrange("p (h d) -> p h d", d=64),
                                        axis=AX.X, op=ALU.add))
        fw.op("dve", [R_st20, R_cst2], [R_st20],
              lambda e: e.scalar_tensor_tensor(out=st20[:, 1, :], in0=st20[:, 0, :], scalar=1.0 / 64, in1=epsT[:],
                                               op0=ALU.mult, op1=ALU.add))
        fw.op("act", [R_st20], [R_st20], lambda e: e.activation(out=st20[:, 2, :], in_=st20[:, 1, :], func=AF.Sqrt))
        fw.op("dve", [R_st20], [R_st20], lambda e: e.reciprocal(out=st20[:, 0, :], in_=st20[:, 2, :]))
        fw.op("dve", [bank[2], bank[3], bank[4], R_st20], [R_qkn],
              lambda e: e.tensor_tensor(out=qkn[:], in0=qk3, in1=bc(st20[:, 0, :].unsqueeze(2), [128, 20, 64]),
                                        op=ALU.mult))
        fw.op("dve", [R_qkn, R_ac], [R_qkn],
              lambda e: e.tensor_tensor(out=qkn[:], in0=qkn[:], in1=gqk[:].rearrange("p (h d) -> p h d", d=64),
                                        op=ALU.mult))
        fw.op("act", [R_qkn], [R_qkbf], lambda e: e.activation(out=qk_bf[:], in_=qkn[:], func=AF.Copy))
        cb = bc(cosT[:, n, :].unsqueeze(1), [128, 20, 8])
        sn = bc(sinT[:, n, :].unsqueeze(1), [128, 20, 8])
        x1v, x2v = qkn[:, :, 0:8], qkn[:, :, 8:16]
        for i, (a_, b_) in enumerate(((x1v, cb), (x2v, sn), (x2v, cb), (x1v, sn))):
            fw.op("dve", [R_qkn, R_ac], [R_rtmp],
                  lambda e, i=i, a_=a_, b_=b_: e.tensor_tensor(out=rtmp[:, i, :, :], in0=a_, in1=b_, op=ALU.mult))
        fw.op("dve", [R_rtmp, R_qkbf], [R_qkbf],
              lambda e: e.tensor_tensor(out=qk_bf[:, :, 0:8], in0=rtmp[:, 0, :, :], in1=rtmp[:, 1, :, :], op=ALU.subtract))
        fw.op("dve", [R_rtmp, R_qkbf], [R_qkbf],
              lambda e: e.tensor_tensor(out=qk_bf[:, :, 8:16], in0=rtmp[:, 2, :, :], in1=rtmp[:, 3, :, :], op=ALU.add))
        for hf in range(2):
            fw.op("act", [R_qkbf], [R_kdup],
                  lambda e, hf=hf: e.activation(out=kdup[:, hf, :, hf * 64:(hf + 1) * 64], in_=qk_bf[:, 16:20, :], func=AF.Copy))
        qflat = qk_bf[:].rearrange("p h d -> p (h d)")
        trq = ps_bf(1)
        for c in range(8):
            fw.op("pe", [R_qkbf, R_const], [bank[1]],
                  lambda e, c=c: e.transpose(out=trq[:, c, :], in_=qflat[:, c * 128:(c + 1) * 128], identity=ident[:]))
        fw.op("act", [bank[1]], [R_qT], lambda e: e.activation(out=qT[:], in_=trq, func=AF.Copy))
        trk = ps_bf(0)
        for c in range(8):
            fw.op("pe", [R_kdup, R_const], [bank[0]],
                  lambda e, c=c: e.transpose(out=trk[:, c, :], in_=kdup[:, c // 4, c % 4, :], identity=ident[:]))
        fw.op("act", [bank[0]], [R_kT[p]], lambda e: e.activation(out=kT[p][:], in_=trk, func=AF.Copy))
        chunks = ([(1 - p, 0)] if n > 0 else []) + [(p, 1)]
        for hkv in range(4):
            pslots = []
            for ci, (slot, mid) in enumerate(chunks):
                b = (5, 0)[ci]
                for g in range(4):
                    hf, pr = g % 2, g // 2
                    fw.op("pe", [R_kT[slot], R_qT], [bank[b]],
                          lambda e, g=g, hf=hf, pr=pr, slot=slot, b=b: e.matmul(
                              out=ps[:, b, g * 128:(g + 1) * 128], lhsT=kT[slot][:, hf * 4 + hkv, :],
                              rhs=qT[:, 2 * hkv + pr, :], start=True, stop=True))
                fw.op("act", [bank[b]], [R_es],
                      lambda e, b=b: e.activation(out=es_t[:].rearrange("p g q -> p (g q)"), in_=ps[:, b, :],
                                                  func=AF.Exp, scale=0.125))
                s = cnt["pT"] % 4
                cnt["pT"] += 1
                fw.op("dve", [R_es, R_ac], [R_pT[s]],
                      lambda e, s=s, mid=mid: e.tensor_tensor(out=pT[s][:], in0=es_t[:],
                                                              in1=bc(masks[:, mid, :].unsqueeze(1), [128, 4, 128]),
                                                              op=ALU.mult))
                pslots.append((s, slot))
            bo = 4 if hkv % 2 == 0 else 1
            for g in range(4):
                for ci, (s, slot) in enumerate(pslots):
                    fw.op("pe", [R_pT[s], R_vx[slot]], [bank[bo]],
                          lambda e, g=g, s=s, slot=slot, ci=ci: e.matmul(
                              out=ps[:, bo, g * 65:(g + 1) * 65], lhsT=pT[s][:, g, :], rhs=vx[slot][:, hkv, 0:65],
                              start=(ci == 0), stop=(ci == len(pslots) - 1)))
            o3 = ps[:, bo, 0:260].rearrange("p (g d) -> p g d", d=65)
            fw.op("dve", [bank[bo], R_esink], [R_den],
                  lambda e, o3=o3: e.tensor_tensor(out=den[:, 0, :].unsqueeze(2), in0=o3[:, :, 64:65],
                                                   in1=esink[:, 4 * hkv:4 * hkv + 4].unsqueeze(2), op=ALU.add))
            fw.op("dve", [R_den], [R_den], lambda e: e.reciprocal(out=den[:, 1, :], in_=den[:, 0, :]))
            fw.op("dve", [bank[bo], R_den], [R_obf],
                  lambda e, o3=o3: e.tensor_tensor(out=o_bf[:, 4 * hkv:4 * hkv + 4, :], in0=o3[:, :, 0:64],
                                                   in1=bc(den[:, 1, :].unsqueeze(2), [128, 4, 64]), op=ALU.mult))
        oflat = o_bf[:].rearrange("p h d -> p (h d)")
        tro = ps_bf(1)
        for c in range(8):
            fw.op("pe", [R_obf, R_const], [bank[1]],
                  lambda e, c=c: e.transpose(out=tro[:, c, :], in_=oflat[:, c * 128:(c + 1) * 128], identity=ident[:]))
        fw.op("act", [bank[1]], [R_oT], lambda e: e.activation(out=oT[:], in_=tro, func=AF.Copy))
        for hf in range(2):
            for c in range(8):
                fw.op("pe", [R_oT, R_wo], [bank[2 + hf]],
                      lambda e, hf=hf, c=c: e.matmul(out=ps[:, 2 + hf, :], lhsT=oT[:, c, :],
                                                     rhs=wo[:, c, hf * 512:(hf + 1) * 512],
                                                     start=(c == 0), stop=(c == 7)))
        fw.op("dve", [bank[2], bank[3], R_X], [R_X1],
              lambda e: e.tensor_tensor(out=X1[:], in0=ps[:, 2:4, :].rearrange("p b f -> p (b f)"), in1=X[:],
                                        op=ALU.add))
    return mix


def conv_setup(fw, nc, sb, ls, dram, ps, bank, ps_bf, L, W, env):
    ident, R_const, gmixT = env["ident"], env["R_const"], env["gmixT"]
    xnT, R_xnT, rms_T = env["xnT"], env["R_xnT"], env["rms_T"]

    win = sb("win", [128, 8, 3072], BF16, ls); R_win = fw.res("win", dma=True)
    wout = sb("wout", [128, 8, 1024], BF16, ls); R_wout = fw.res("wout", dma=True)
    for kc in range(8):
        for c in range(3):
            fw.dma("pool", R_win, [], [R_win],
                   lambda e, kc=kc, c=c: e.dma_start(out=win[:, kc, c * 1024:(c + 1) * 1024],
                                                     in_=dram["win"][kc * 128:(kc + 1) * 128, c * 1024:(c + 1) * 1024]),
                   multi=True)
        fw.dma("pool", R_wout, [], [R_wout],
               lambda e, kc=kc: e.dma_start(out=wout[:, kc, :], in_=dram["wout"][kc * 128:(kc + 1) * 128, :]),
               multi=True)
    cwT = sb("cwT", [128, 3, 8], F32, ls); R_cw = fw.res("cwT", dma=True)
    fw.dma("sp", R_cw, [], [R_cw], lambda e: e.dma_start(out=cwT[:], in_=dram["cwT"]))
    zb = [sb("zb%d" % i, [128, 8, 130], F32, ls) for i in range(2)]
    R_zb = [fw.res("zb%d" % i) for i in range(2)]
    u_sb = sb("u_sb", [128, 8, 128], F32, ls); R_u = fw.res("u_sb")
    t0, R_t0 = u_sb, R_u
    yb = sb("yb", [128, 8, 128], BF16, ls); R_yb = fw.res("yb")
    fw.op("dve", [], [R_zb[0]], lambda e: e.memset(zb[0][:], 0.0))

    def mix(n, X, R_X, X1, R_X1):
        p = n % 2
        rms_T(X[:], R_X, gmixT[:, L, :], R_const)
        for j in range(24):
            b = (1, 2, 3, 4, 5, 0)[j // 4]
            for kc in range(8):
                fw.op("pe", [R_xnT, R_win], [bank[b]],
                      lambda e, j=j, kc=kc, b=b: e.matmul(out=ps[:, b, (j % 4) * 128:(j % 4 + 1) * 128],
                                                          lhsT=win[:, kc, j * 128:(j + 1) * 128], rhs=xnT[:, kc, :],
                                                          start=(kc == 0), stop=(kc == 7)))
        v3 = lambda b0: ps[:, b0:b0 + 2, :].rearrange("p b (c t) -> p (b c) t", t=128)
        v1 = lambda b0: ps[:, b0, :].rearrange("p (c t) -> p c t", t=128)
        fw.op("act", [bank[5]], [R_u], lambda e: e.activation(out=u_sb[:, 0:4, :], in_=v1(5), func=AF.Copy))
        fw.op("act", [bank[0], R_u], [R_u], lambda e: e.activation(out=u_sb[:, 4:8, :], in_=v1(0), func=AF.Copy))
        fw.op("dve", [bank[3], bank[4], R_u], [R_zb[p]],
              lambda e: e.tensor_tensor(out=zb[p][:, :, 2:130], in0=v3(3), in1=u_sb[:], op=ALU.mult))
        fw.op("act", [R_zb[p]], [R_zb[1 - p]],
              lambda e: e.activation(out=zb[1 - p][:, :, 0:2], in_=zb[p][:, :, 128:130], func=AF.Copy))
        fw.op("dve", [R_zb[p], R_cw], [R_t0],
              lambda e: e.tensor_tensor(out=t0[:], in0=zb[p][:, :, 0:128],
                                        in1=bc(cwT[:, 0, :].unsqueeze(2), [128, 8, 128]), op=ALU.mult))
        for j in (1, 2):
            for cc in range(8):
                fw.op("dve", [R_zb[p], R_cw, R_t0], [R_t0],
                      lambda e, j=j, cc=cc: e.scalar_tensor_tensor(
                          out=t0[:, cc, :], in0=zb[p][:, cc, j:j + 128], scalar=cwT[:, j, cc:cc + 1],
                          in1=t0[:, cc, :], op0=ALU.mult, op1=ALU.add))
        fw.op("dve", [R_t0, bank[1], bank[2]], [R_yb],
              lambda e: e.tensor_tensor(out=yb[:], in0=v3(1), in1=t0[:], op=ALU.mult))
        for hf in range(2):
            for c in range(8):
                fw.op("pe", [R_yb, R_wout], [bank[(5, 0)[hf]]],
                      lambda e, hf=hf, c=c: e.matmul(out=ps[:, (5, 0)[hf], :], lhsT=yb[:, c, :],
                                                     rhs=wout[:, c, hf * 512:(hf + 1) * 512],
                                                     start=(c == 0), stop=(c == 7)))
        for hf in range(2):
            fw.op("dve", [bank[(5, 0)[hf]], R_X, R_X1], [R_X1],
                  lambda e, hf=hf: e.tensor_tensor(out=X1[:, hf * 512:(hf + 1) * 512], in0=ps[:, (5, 0)[hf], :],
                                                   in1=X[:, hf * 512:(hf + 1) * 512], op=ALU.add))
    return mix


def _consts():
    ident = np.eye(128, dtype=np.float32).astype(ml_dtypes.bfloat16)
    k = np.arange(128)[:, None]
    q = np.arange(128)[None, :]
    masks = np.stack([(k > q), (k <= q)], axis=1).astype(np.float32).astype(ml_dtypes.bfloat16)
    pos = (np.arange(NB)[None, :] * 128 + np.arange(128)[:, None]).astype(np.float32)
    freqs = (500000.0 ** (-np.arange(0, 16, 2, dtype=np.float32) / 16)).astype(np.float32)
    ang = pos[:, :, None] * freqs[None, None, :]
    iota16 = np.broadcast_to(np.arange(16, dtype=np.float32)[None, :], (128, 16)).copy()
    return dict(ident=ident, masks=np.ascontiguousarray(masks),
                ropecos=np.cos(ang).astype(np.float32), ropesin=np.sin(ang).astype(np.float32), iota16=iota16)


def make_in_maps(inp, ncores=8, nblk=NB):
    f = lambda a: np.ascontiguousarray(np.asarray(a, dtype=np.float32))
    colT = lambda g: np.ascontiguousarray(f(g).reshape(2, 8, 128).transpose(2, 0, 1))
    shared = dict(
        wqkv=f(inp["attn_w_qkv"][0]), wo=f(inp["attn_w_o"][0]),
        win=f(inp["conv_w_in"][0]), wout=f(inp["conv_w_out"][0]),
        wq0=f(inp["peer_w_query"][0]), wq1=f(inp["peer_w_query"][1]),
        skT0=np.ascontiguousarray(f(inp["peer_sub_keys"][0]).reshape(16, 128, 128).transpose(2, 0, 1).reshape(128, 2048)),
        skT1=np.ascontiguousarray(f(inp["peer_sub_keys"][1]).reshape(16, 128, 128).transpose(2, 0, 1).reshape(128, 2048)),
        puv0=np.concatenate([f(inp["peer_u"][0]), f(inp["peer_v"][0])], axis=1),
        puv1=np.concatenate([f(inp["peer_u"][1]), f(inp["peer_v"][1])], axis=1),
        gmixT=colT(inp["norm_mix"]), gffnT=colT(inp["norm_ffn"]), gffn=f(inp["norm_ffn"]),
        gqk=np.concatenate([np.tile(f(inp["attn_q_norm"][0]), 16), np.tile(f(inp["attn_k_norm"][0]), 4)]),
        sinks=f(inp["attn_sinks"][0]),
        cwT=np.ascontiguousarray(f(inp["conv_w"][0]).reshape(3, 8, 128).transpose(2, 0, 1)),
    )
    shared.update(_consts())
    x = f(inp["x"])
    maps = []
    for c in range(ncores):
        m = dict(shared)
        m["x"] = np.ascontiguousarray(x[c, :nblk * 128])
        maps.append(m)
    return maps


def kernel(**inputs):
    nc = build()
    maps = make_in_maps(inputs)
    res = run_bass_kernel_spmd(nc, maps, core_ids=list(range(8)))
    return np.stack([np.asarray(r["out"], dtype=np.float32) for r in res.results], axis=0)
```
